# Optimizing a Trainium2 kernel written in Bass

```python
import jax, jax.numpy as jnp
from jax import lax
import numpy as np

D_MODEL = 1024
BATCH = 16
SEQ = 2048
DEPTH = 2

CHUNK = 64
EPS = 1e-6

POOL_WINDOWS = (2, 4, 8, 16)
POOL_WIDTH = D_MODEL
POOL_GROUP = POOL_WIDTH // len(POOL_WINDOWS)

LRU_WIDTH = D_MODEL
LRU_BLOCKS = 16
LRU_BLOCK = LRU_WIDTH // LRU_BLOCKS
CONV_WIDTH = 4
LRU_C = 8.0

SB_HEADS = 8
SB_HEAD_DIM = 128
SB_WIDTH = SB_HEADS * SB_HEAD_DIM
Q_BLOCK = 128

N_BRANCH = 3
IN_COLS = POOL_WIDTH + LRU_WIDTH + 3 * SB_WIDTH + N_BRANCH * D_MODEL
D_FF = ((8 * D_MODEL + 3 * 256 - 1) // (3 * 256)) * 256

kernel_name = "hybrid_pool_rglru_stickbreak_block"


def rms_norm(x, g):
    x32 = x.astype(jnp.float32)
    y = x32 * lax.rsqrt(jnp.mean(x32 * x32, axis=-1, keepdims=True) + EPS)
    return (y * g.astype(jnp.float32)).astype(x.dtype)


def pool_mixer(u, w_pool, pool_scale):
    S = u.shape[1]
    u32 = u.astype(jnp.float32)
    pos = jnp.arange(1, S + 1, dtype=jnp.float32)[None, :, None]
    outs = []
    for g, w in enumerate(POOL_WINDOWS):
        ug = u32[..., g * POOL_GROUP:(g + 1) * POOL_GROUP]
        cs = jnp.cumsum(ug, axis=1)
        cs_lag = jnp.pad(cs, ((0, 0), (w, 0), (0, 0)))[:, :S]
        mean = (cs - cs_lag) / jnp.minimum(pos, float(w))
        outs.append(mean - ug)
    p = jnp.stack(outs, axis=2).astype(u.dtype)
    y = jnp.einsum('bsgi,gij->bsgj', p, w_pool)
    return y.reshape(u.shape) * pool_scale


def rglru_mixer(u, conv_w, conv_b, w_rg, b_rg, w_ig, b_ig, lru_lambda):
    B, S, _ = u.shape
    up = jnp.pad(u, ((0, 0), (CONV_WIDTH - 1, 0), (0, 0)))
    xc = conv_b + conv_w[0] * up[:, 0:S]
    for k in range(1, CONV_WIDTH):
        xc = xc + conv_w[k] * up[:, k:k + S]
    xb = xc.reshape(B, S, LRU_BLOCKS, LRU_BLOCK)
    r = jax.nn.sigmoid(jnp.einsum('bshi,hij->bshj', xb, w_rg).reshape(B, S, LRU_WIDTH) + b_rg)
    i = jax.nn.sigmoid(jnp.einsum('bshi,hij->bshj', xb, w_ig).reshape(B, S, LRU_WIDTH) + b_ig)
    log_a = (-LRU_C * r.astype(jnp.float32)) * jax.nn.softplus(-lru_lambda.astype(jnp.float32))
    a = jnp.exp(log_a)
    mult = jnp.sqrt(-jnp.expm1(2.0 * log_a))
    b = mult * (i * xc).astype(jnp.float32)

    def combine(c1, c2):
        a1, b1 = c1
        a2, b2 = c2
        return a1 * a2, a2 * b1 + b2

    _, h = lax.associative_scan(combine, (a, b), axis=1)
    return h.astype(u.dtype)


def stick_breaking_attention(q, k, v):
    B, S, H, Dh = q.shape
    scale = Dh ** -0.5
    outs = []
    for start in range(0, S, Q_BLOCK):
        end = start + Q_BLOCK
        qb = q[:, start:end]
        kb = k[:, :end]
        vb = v[:, :end]
        z = jnp.einsum('bqhd,bkhd->bhqk', qb, kb).astype(jnp.float32) * scale
        qpos = jnp.arange(start, end)[:, None]
        kpos = jnp.arange(end)[None, :]
        causal = kpos < qpos
        log_beta = jax.nn.log_sigmoid(z)
        log_keep = jnp.where(causal, jax.nn.log_sigmoid(-z), 0.0)
        after = lax.cumsum(log_keep, axis=3, reverse=True) - log_keep
        w = jnp.where(causal, jnp.exp(log_beta + after), 0.0)
        outs.append(jnp.einsum('bhqk,bkhd->bqhd', w.astype(v.dtype), vb))
    return jnp.concatenate(outs, axis=1)


def setup_inputs(seed: int = 0) -> dict:
    key = jax.random.key(seed)
    ks = jax.random.split(key, 24)
    f32 = jnp.float32
    nrm = lambda k, shape, s: jax.random.normal(k, shape, f32) * s
    u = jax.random.uniform(ks[13], (DEPTH, LRU_WIDTH), f32, minval=0.9, maxval=0.999)
    p = u ** (1.0 / LRU_C)
    lru_lambda = jnp.log(p) - jnp.log1p(-p)
    return {
        "x": nrm(ks[0], (BATCH, SEQ, D_MODEL), 1.0),
        "norm_mix": 1.0 + nrm(ks[1], (DEPTH, D_MODEL), 0.02),
        "w_in": nrm(ks[2], (DEPTH, D_MODEL, IN_COLS), D_MODEL ** -0.5),
        "b_gate": nrm(ks[3], (DEPTH, N_BRANCH * D_MODEL), 0.01),
        "w_pool": nrm(ks[4], (DEPTH, len(POOL_WINDOWS), POOL_GROUP, POOL_GROUP), POOL_GROUP ** -0.5),
        "pool_scale": 1.0 + nrm(ks[5], (DEPTH, POOL_WIDTH), 0.1),
        "conv_w": nrm(ks[6], (DEPTH, CONV_WIDTH, LRU_WIDTH), CONV_WIDTH ** -0.5),
        "conv_b": nrm(ks[7], (DEPTH, LRU_WIDTH), 0.01),
        "w_rg": nrm(ks[8], (DEPTH, LRU_BLOCKS, LRU_BLOCK, LRU_BLOCK), LRU_BLOCK ** -0.5),
        "b_rg": nrm(ks[9], (DEPTH, LRU_WIDTH), 0.01),
        "w_ig": nrm(ks[10], (DEPTH, LRU_BLOCKS, LRU_BLOCK, LRU_BLOCK), LRU_BLOCK ** -0.5),
        "b_ig": nrm(ks[11], (DEPTH, LRU_WIDTH), 0.01),
        "lru_lambda": lru_lambda,
        "q_norm": 1.0 + nrm(ks[14], (DEPTH, SB_HEAD_DIM), 0.02),
        "k_norm": 1.0 + nrm(ks[15], (DEPTH, SB_HEAD_DIM), 0.02),
        "w_branch": nrm(ks[16], (DEPTH, N_BRANCH, D_MODEL, D_MODEL), D_MODEL ** -0.5),
        "w_out": nrm(ks[17], (DEPTH, D_MODEL, D_MODEL), D_MODEL ** -0.5),
        "norm_ffn": 1.0 + nrm(ks[18], (DEPTH, D_MODEL), 0.02),
        "w_ffn_gate": nrm(ks[19], (DEPTH, D_MODEL, D_FF), D_MODEL ** -0.5),
        "w_ffn_up": nrm(ks[20], (DEPTH, D_MODEL, D_FF), D_MODEL ** -0.5),
        "w_ffn_down": nrm(ks[21], (DEPTH, D_FF, D_MODEL), D_FF ** -0.5),
    }


def reference(x, norm_mix, w_in, b_gate, w_pool, pool_scale, conv_w, conv_b, w_rg, b_rg,
              w_ig, b_ig, lru_lambda, q_norm, k_norm, w_branch, w_out, norm_ffn,
              w_ffn_gate, w_ffn_up, w_ffn_down):
    B, S, _ = x.shape
    splits = np.cumsum([POOL_WIDTH, LRU_WIDTH, SB_WIDTH, SB_WIDTH, SB_WIDTH]).tolist()
    for l in range(DEPTH):
        h = rms_norm(x, norm_mix[l])
        proj = jnp.einsum('bsd,dc->bsc', h, w_in[l])
        u_pool, u_lru, q, k, v, g_logits = jnp.split(proj, splits, axis=-1)
        gates = jax.nn.sigmoid(g_logits + b_gate[l]).reshape(B, S, N_BRANCH, D_MODEL)

        y_pool = pool_mixer(u_pool, w_pool[l], pool_scale[l])
        y_lru = rglru_mixer(u_lru, conv_w[l], conv_b[l], w_rg[l], b_rg[l],
                            w_ig[l], b_ig[l], lru_lambda[l])
        qh = rms_norm(q.reshape(B, S, SB_HEADS, SB_HEAD_DIM), q_norm[l])
        kh = rms_norm(k.reshape(B, S, SB_HEADS, SB_HEAD_DIM), k_norm[l])
        vh = v.reshape(B, S, SB_HEADS, SB_HEAD_DIM)
        y_sb = stick_breaking_attention(qh, kh, vh).reshape(B, S, SB_WIDTH)

        merged = (gates[:, :, 0] * jnp.einsum('bsc,cd->bsd', y_pool, w_branch[l, 0])
                  + gates[:, :, 1] * jnp.einsum('bsc,cd->bsd', y_lru, w_branch[l, 1])
                  + gates[:, :, 2] * jnp.einsum('bsc,cd->bsd', y_sb, w_branch[l, 2]))
        x = x + jnp.einsum('bsd,de->bse', merged, w_out[l])

        h2 = rms_norm(x, norm_ffn[l])
        act = jax.nn.silu(jnp.einsum('bsd,df->bsf', h2, w_ffn_gate[l])) * jnp.einsum('bsd,df->bsf', h2, w_ffn_up[l])
        x = x + jnp.einsum('bsf,fd->bsd', act, w_ffn_down[l])
    return x
```

```python
import numpy as np
from contextlib import ExitStack
import concourse.bass as bass
import concourse.mybir as mybir
from concourse.bass_utils import run_bass_kernel_spmd

F32 = mybir.dt.float32
BF16 = mybir.dt.bfloat16
AF = mybir.ActivationFunctionType
ALU = mybir.AluOpType

D = 1024
SEQ = 2048
T = 512
DFF = 2816
NFC = 22
EPS = 1e-6
WINDOWS = (2, 4, 8, 16)
EPOCH = 6000


class Op:
    __slots__ = ("eng", "fn", "is_dma", "signal", "deps", "need_sig", "waits", "inc", "occ", "lat", "idx", "tag", "st", "crit", "rdep")

    def __init__(self, eng, fn, is_dma, signal, occ=300.0, lat=None):
        self.occ = occ
        self.lat = occ if lat is None else lat
        self.eng = eng
        self.fn = fn
        self.is_dma = is_dma
        self.signal = signal
        self.deps = {}
        self.need_sig = False
        self.waits = []
        self.inc = 16 if is_dma else 1


class Prog:
    def __init__(self):
        self.ops = []
        self.pages = {}

    def add(self, eng, fn, reads=(), writes=(), dma_signal=None, occ=300.0, lat=None):
        op = Op(eng, fn, dma_signal is not None, dma_signal, occ, lat)
        op.tag = getattr(self, "phase", "")
        op.rdep = None
        deps = op.deps
        is_dma = dma_signal is not None
        for k in reads:
            st = self.pages.get(k)
            if st is None:
                st = self.pages[k] = [{}, {}]
            for w in st[0].values():
                deps[id(w)] = (w, True)
            if fn is not None:
                rkey = id(op) if is_dma else eng
                prev = st[1].get(rkey)
                if prev is not None and prev is not op and id(prev) not in deps:
                    deps[id(prev)] = (prev, None)
                st[1][rkey] = op
        for k in writes:
            st = self.pages.get(k)
            if st is None:
                st = self.pages[k] = [{}, {}]
            for w in st[0].values():
                if id(w) not in deps:
                    deps[id(w)] = (w, False)
            for r in st[1].values():
                if r is not op and (id(r) not in deps or deps[id(r)][1] is None):
                    deps[id(r)] = (r, False)
            st[1] = {}
            wkey = ("dma", id(dma_signal[0])) if is_dma else eng
            st[0][wkey] = op
        self.ops.append(op)
        return op

    def schedule(self, sync_lat=250.0):
        import heapq
        ops = self.ops
        n = len(ops)
        for i, op in enumerate(ops):
            op.idx = i
        succ = [[] for _ in range(n)]
        indeg = [0] * n
        for op in ops:
            for d, raw in op.deps.values():
                succ[d.idx].append(op.idx)
                indeg[op.idx] += 1
        prio = [0.0] * n
        if getattr(self, "use_blevel", False):
            bl = [0.0] * n
            for i in range(n - 1, -1, -1):
                m_ = 0.0
                for j in succ[i]:
                    if bl[j] > m_:
                        m_ = bl[j]
                bl[i] = m_ + ops[i].lat + 100.0
            for i in range(n):
                prio[i] = -bl[i]
        else:
            for i in range(n):
                prio[i] = float(i)
        engs = sorted(set(op.eng for op in ops))
        free = {e: 0.0 for e in engs}
        ready = {e: [] for e in engs}
        avail = {e: [] for e in engs}
        finish = [0.0] * n
        efree = [0.0] * n
        rtime = [0.0] * n
        order = {e: [] for e in engs}
        for op in ops:
            if indeg[op.idx] == 0:
                heapq.heappush(ready[op.eng], (0.0, op.idx))
        done = 0
        while done < n:
            best = None
            for e in engs:
                rq, aq = ready[e], avail[e]
                while rq and rq[0][0] <= free[e]:
                    i_ = heapq.heappop(rq)[1]
                    heapq.heappush(aq, (prio[i_], i_))
                if aq:
                    cand = (free[e], aq[0][1], e, True)
                elif rq:
                    cand = (rq[0][0], rq[0][1], e, False)
                else:
                    continue
                if best is None or cand < best:
                    best = cand
            st, i, e, from_avail = best
            if from_avail:
                heapq.heappop(avail[e])
            else:
                heapq.heappop(ready[e])
            op = ops[i]
            op.st = st
            prev_on_eng = order[e][-1] if order[e] else None
            if prev_on_eng is not None and free[e] >= rtime[i]:
                op.crit = ("res", prev_on_eng)
            else:
                op.crit = ("dep", getattr(op, "rdep", None))
            free[e] = st + op.occ
            efree[i] = st + op.occ
            finish[i] = st + op.lat
            order[e].append(op)
            done += 1
            for j in succ[i]:
                o2 = ops[j]
                same = (o2.eng == op.eng and not op.is_dma and not o2.is_dma)
                if same and (op.eng == "pe" or o2.deps[id(op)][1] is None):
                    t = efree[i]
                elif same:
                    t = finish[i] + 80.0
                else:
                    t = finish[i] + sync_lat
                if t > rtime[j]:
                    rtime[j] = t
                    o2.rdep = op
                indeg[j] -= 1
                if indeg[j] == 0:
                    heapq.heappush(ready[o2.eng], (rtime[j], j))
        nfill = 0
        ff = getattr(self, "filler", None)
        if ff is not None:
            frac, cap, mingap, fn_ = ff
            newpe = []
            prev_end = 0.0
            for op in order["pe"]:
                gap = op.st - prev_end
                if gap > mingap and op.st > 60000.0:
                    k = min(int(frac * gap / 220.0), cap)
                    for _ in range(k):
                        fo = Op("pe", fn_, False, None, 216.0, 216.0)
                        fo.tag = "filler"
                        fo.st = prev_end
                        newpe.append(fo)
                        nfill += 1
                newpe.append(op)
                prev_end = op.st + op.occ
            order["pe"] = newpe
        self.nfill = nfill
        self.order = order
        self.makespan = max(finish) if n else 0.0
        return order

    def finalize(self, new_sem, reorder=True):
        if reorder:
            self.schedule(getattr(self, 'sync_lat', 250.0))
            newops = []
            for e in self.order:
                newops.extend(self.order[e])
            self.ops = newops
        for op in self.ops:
            for d, raw in op.deps.values():
                if raw is None:
                    continue
                if d.is_dma and op.is_dma and d.signal[0] is op.signal[0] and d.signal[1] == op.signal[1]:
                    continue
                if d.eng == op.eng and not d.is_dma and not op.is_dma:
                    if op.eng == "pe":
                        continue
                d.need_sig = True
                op.waits.append(d)
        cnt = {}
        sems = {}
        for op in self.ops:
            if op.is_dma or not op.need_sig:
                continue
            n = cnt.get(op.eng, 0)
            cnt[op.eng] = n + 1
            key = (op.eng, n // EPOCH)
            if key not in sems:
                sems[key] = new_sem("s_%s_%d" % key)
            op.signal = (sems[key], n % EPOCH + 1)

    def emit(self, eng_name, e):
        waited = {}
        for op in self.ops:
            if op.eng != eng_name:
                continue
            for d in op.waits:
                sem, v = d.signal
                if waited.get(id(sem), 0) < v:
                    e.wait_ge(sem, v)
                    waited[id(sem)] = v
            if op.fn is None:
                continue
            ins = op.fn(e)
            if op.is_dma or op.need_sig:
                ins.then_inc(op.signal[0], op.inc)


class Rot:
    def __init__(self, items):
        self.items = items
        self.i = 0

    def next(self):
        it = self.items[self.i % len(self.items)]
        self.i += 1
        return it


def build(NSEQ=2, NL=2, NT=4, NSLOT=3, NSC=7, NSB=4, whatif=False, blevel=False, PIPE=True, DEEP=True, FILL=(0.6, 12, 500.0), FILL_BANK=5, sync_lat=320.0, PSCFG=([0, 1, 2, 3, 6, 7], [0, 1], [2, 3], [4]), speed=None, emit=True, PE_A=8.0, PE_B=0.417):
    nc = bass.Bass("TRN2", target_bir_lowering=False)
    S_ = NT * T
    dram = {}

    def din(name, shape):
        dram[name] = nc.dram_tensor(name, list(shape), F32, kind="ExternalInput").ap()
        return dram[name]

    x = din("x", [NSEQ, S_, D])
    norm_mix = din("norm_mix", [NL, D])
    w_in = din("w_in", [NL, D, 8192])
    b_gate = din("b_gate", [NL, 3 * D])
    w_pool = din("w_pool", [NL, 4, 256, 256])
    pool_scale = din("pool_scale", [NL, D])
    conv_w = din("conv_w", [NL, 4, D])
    conv_b = din("conv_b", [NL, D])
    w_rg = din("w_rg", [NL, 16, 64, 64])
    b_rg = din("b_rg", [NL, D])
    w_ig = din("w_ig", [NL, 16, 64, 64])
    b_ig = din("b_ig", [NL, D])
    lru_lambda = din("lru_lambda", [NL, D])
    q_norm = din("q_norm", [NL, 128])
    k_norm = din("k_norm", [NL, 128])
    w_branch = din("w_branch", [NL, 3, D, D])
    w_out = din("w_out", [NL, D, D])
    norm_ffn = din("norm_ffn", [NL, D])
    w_ffn_gate = din("w_ffn_gate", [NL, D, DFF])
    w_ffn_up = din("w_ffn_up", [NL, D, DFF])
    w_ffn_down = din("w_ffn_down", [NL, DFF, D])
    y = nc.dram_tensor("y", [NSEQ, S_, D], F32, kind="ExternalOutput").ap()
    dbg_out = {}

    def scr(name, shape):
        return nc.dram_tensor(name, list(shape), BF16, kind="Internal").ap()

    WIN = scr("s_win", [NL, 20, 128, 2048])
    WPO = scr("s_wpo", [NL, 4, 128, 512])
    WBD = scr("s_wbd", [NL, 4, 128, 512])
    WMG = scr("s_wmg", [NL, 8, 3, 128, 2048])
    WOU = scr("s_wou", [NL, 4, 128, 2048])
    WGU = scr("s_wgu", [NL, NFC, 128, 2048])
    WDN = scr("s_wdn", [NL, 8, 2, 128, 1408])

    P = Prog()
    es = ExitStack()

    def sb(name, shape, dt):
        return es.enter_context(nc.sbuf_tensor(name, list(shape), dt))

    sem_list = []

    def new_sem(name):
        s = es.enter_context(nc.semaphore(name))
        sem_list.append(s)
        return s

    XT = sb("XT", [128, 8, SEQ], BF16 if whatif else F32)
    KT = sb("KT", [128, 8, SEQ], BF16)
    VC = sb("VC", [128, 16, 1024], BF16)
    HT = sb("HT", [128, 8, T], BF16)
    UY = sb("UY", [128, 24, T], BF16)
    PQ = sb("PQ", [128, 4, T], BF16)
    MB = sb("MB", [128, 8, T], BF16)
    WS = sb("WS", [128, NSLOT, 2048], BF16)
    PV = sb("PV", [128, NL, 132], F32)
    IDT = sb("IDT", [128, 128], F32)
    CONB = sb("CONB", [128, 4, 128], BF16)
    RSUM = sb("RSUM", [128, 2, T], BF16)
    ZB = RSUM[:, 0, :]
    RC = sb("RC", [128, 4, 16], F32)
    UPH = sb("UPH", [128, 8, 15], F32)
    ULH = sb("ULH", [128, 8, 4], F32)
    HST = sb("HST", [128, 8], F32)
    SCF = sb("SC", [128, NSC * 528], F32)

    class _SC:
        def __getitem__(self, idx):
            p_, i_, c_ = idx
            assert isinstance(i_, int)
            return SCF[p_, i_ * 528:(i_ + 1) * 528][:, c_]
    SC = _SC()

    class _STG:
        def __getitem__(self, idx):
            p_, b_, c_ = idx
            return SCF[p_, b_ * 1056:b_ * 1056 + 1024][:, c_]
    STG = _STG()
    STGK = lambda b: [("sc", 2 * b), ("sc", 2 * b + 1)]
    SCB = sb("SCB", [128, NSB, T], BF16)

    psum = [es.enter_context(nc.psum_tensor("ps%d" % i, [128, 512], F32)) for i in range(8)]

    def psrot(ids):
        return Rot([(psum[i], ("ps", i)) for i in ids])

    PG = psrot(PSCFG[0])
    PZ = psrot(PSCFG[1])
    PA = psrot(PSCFG[2])
    PO = psrot(PSCFG[3])
    PX = psrot([6, 7])

    IOT = SC[:, 3, 0:16]
    ONESF = SC[:, 0, 0:128]
    IDENT = IDT[:, :]
    MASKF = SC[:, 1, 0:128]
    UF = SC[:, 2, 0:128]
    ONESB = CONB[:, 0, :]
    NEGB = CONB[:, 1, :]
    UMAT = CONB[:, 2, :]
    IDENTB = CONB[:, 3, :]

    def fsz(ap_):
        n_ = 1
        for d_ in ap_.shape[1:]:
            n_ *= int(d_)
        return n_

    def mm(out, lhsT, rhs, start, stop, reads, writes, skip=False):
        n_ = fsz(rhs)
        P.add("pe", lambda e: e.matmul(out, lhsT, rhs, start=start, stop=stop, skip_group_check=skip),
              reads, writes, occ=PE_A + max(n_, 64) * PE_B, lat=200.0 + n_ * PE_B)

    def act(out, in_, func, reads, writes, bias=None, scale=None):
        kw = {}
        if bias is not None:
            kw["bias"] = bias
        if scale is not None:
            kw["scale"] = scale
        P.add("act", lambda e: e.activation(out, in_, func, **kw), reads, writes, occ=150.0 + fsz(out) * 0.75)

    def sigmoid3(out, in_, negb, rin, wkey, extra):
        act(out, in_, AF.Exp, rin + extra, [wkey], bias=negb, scale=-1.0)
        act(out, out, AF.Ln, [wkey], [wkey], bias=1.0)
        act(out, out, AF.Exp, [wkey], [wkey], scale=-1.0)

    def tt(eng, out, in0, in1, op, reads, writes):
        P.add(eng, lambda e: e.tensor_tensor(out, in0, in1, op), reads, writes,
              occ=(100.0 + fsz(out) * 0.95) if eng == "dve" else (200.0 + fsz(out) * 1.66))

    def stt(out, in0, scalar, in1, op0, op1, reads, writes):
        P.add("dve", lambda e: e.scalar_tensor_tensor(out, in0, scalar, in1, op0, op1), reads, writes,
              occ=160.0 + fsz(out) * 1.05)

    def ts(eng, out, in0, s1, s2, op0, op1, reads, writes):
        oc_ = (160.0 + fsz(out) * 0.6) if eng == "dve" else (250.0 + fsz(out) * 1.7)
        if op1 is None:
            P.add(eng, lambda e: e.tensor_scalar(out, in0, s1, None, op0), reads, writes, occ=oc_)
        else:
            P.add(eng, lambda e: e.tensor_scalar(out, in0, s1, s2, op0, op1), reads, writes, occ=oc_)

    def cp(eng, out, in_, reads, writes):
        if eng == "act":
            P.add("act", lambda e: e.copy(out, in_), reads, writes, occ=260.0 + fsz(out) * 0.85)
        else:
            P.add(eng, lambda e: e.tensor_copy(out, in_), reads, writes,
                  occ=(160.0 + fsz(out) * 0.8) if eng == "dve" else (250.0 + fsz(out) * 1.5))

    def memset(eng, ap, val, writes):
        P.add(eng, lambda e: e.memset(ap, val), (), writes, occ=160.0 + fsz(ap) * 1.0)

    def dma(q, out, in_, reads, writes, signal, **kw):
        nel = 1
        for d_ in out.shape:
            nel *= int(d_)
        P.add(q, lambda e: e.dma_start(out=out, in_=in_, **kw), reads, writes, dma_signal=signal,
              occ=(1500.0 if q == "pool" else 120.0), lat=2200.0 + nel * 2 * 0.006)

    CK = [("sc", 0)]
    memset("pool", ONESF, 1.0, CK)
    P.add("pool", lambda e: e.affine_select(IDENT, ONESF, [[1, 128]], ALU.is_equal, 0.0, base=0,
                                            channel_multiplier=-1), CK, ["c_id"])
    P.add("pool", lambda e: e.affine_select(MASKF, ONESF, [[1, 128]], ALU.is_gt, 0.0, base=0,
                                            channel_multiplier=-1), CK, [("sc", 1)])
    P.add("pool", lambda e: e.affine_select(UF, ONESF, [[-1, 128]], ALU.is_gt, 0.0, base=0,
                                            channel_multiplier=1), CK, [("sc", 2)])
    P.add("pool", lambda e: e.iota(IOT, [[1, 16]], base=1, channel_multiplier=0,
                                   allow_small_or_imprecise_dtypes=True), (), [("sc", 3)])
    memset("pool", ZB, 0.0, [("rsum", 0)])
    cp("dve", ONESB, ONESF, CK, ["c_onesb"])
    ts("dve", NEGB, MASKF, 28.0, -28.0, ALU.mult, ALU.add, [("sc", 1)], ["c_negb"])
    cp("dve", IDENTB, IDENT, ["c_id"], ["c_idb"])
    cp("dve", UMAT, UF, [("sc", 2)], ["c_umat"])
    for g, w in enumerate(WINDOWS):
        ts("dve", RC[:, g, :], IOT, float(w), None, ALU.min, None, [("sc", 3)], [("c_rc", g)])
        P.add("dve", lambda e, g=g: e.reciprocal(RC[:, g, :], RC[:, g, :]), [("c_rc", g)], [("c_rc", g)])

    C_NM, C_PS, C_CW, C_CB, C_BRG, C_BIG, C_LAM, C_NF, C_BG, C_QN, C_KN, C_NSP, C_NSP2, C_QG, C_TMP = (
        0, 8, 16, 48, 56, 64, 72, 80, 88, 112, 113, 114, 122, 130, 72)
    psem = new_sem("psem")
    pl = []
    for l in range(NL):
        def vec8(src):
            return src.rearrange("(c p) -> p c", p=128)
        pl.append((PV[:, l, C_NM:C_NM + 8], vec8(norm_mix[l])))
        pl.append((PV[:, l, C_PS:C_PS + 8], vec8(pool_scale[l])))
        for k in range(4):
            pl.append((PV[:, l, C_CW + 8 * k:C_CW + 8 * k + 8], vec8(conv_w[l, k])))
        pl.append((PV[:, l, C_CB:C_CB + 8], vec8(conv_b[l])))
        pl.append((PV[:, l, C_BRG:C_BRG + 8], vec8(b_rg[l])))
        pl.append((PV[:, l, C_BIG:C_BIG + 8], vec8(b_ig[l])))
        pl.append((PV[:, l, C_LAM:C_LAM + 8], vec8(lru_lambda[l])))
        pl.append((PV[:, l, C_NF:C_NF + 8], vec8(norm_ffn[l])))
        pl.append((PV[:, l, C_BG:C_BG + 24], b_gate[l].rearrange("(c p) -> p c", p=128)))
        pl.append((PV[:, l, C_QN:C_QN + 1], q_norm[l].rearrange("(p o) -> p o", o=1)))
        pl.append((PV[:, l, C_KN:C_KN + 1], k_norm[l].rearrange("(p o) -> p o", o=1)))
    ptotal = 16 * len(pl)
    for o_, i_ in pl:
        dma("sp", o_, i_, (), ["pv_raw"], (psem, ptotal), allow_slow_non_contiguous=True)
    for l in range(NL):
        act(PV[:, l, C_TMP:C_TMP + 8], PV[:, l, C_LAM:C_LAM + 8], AF.Exp, ["pv_raw"], [("pv_t", l)], scale=-1.0)
        act(PV[:, l, C_TMP:C_TMP + 8], PV[:, l, C_TMP:C_TMP + 8], AF.Ln, [("pv_t", l)], [("pv_t", l)], bias=1.0)
        ts("dve", PV[:, l, C_NSP:C_NSP + 8], PV[:, l, C_TMP:C_TMP + 8], -8.0, None, ALU.mult, None,
           [("pv_t", l)], [("pv_d", l)])
        ts("dve", PV[:, l, C_NSP2:C_NSP2 + 8], PV[:, l, C_TMP:C_TMP + 8], -16.0, None, ALU.mult, None,
           [("pv_t", l)], [("pv_d", l)])
        ts("dve", PV[:, l, C_QG:C_QG + 1], PV[:, l, C_QN:C_QN + 1], 128.0 ** -0.5, None, ALU.mult, None,
           ["pv_raw"], [("pv_d", l)])
        ts("dve", PV[:, l, C_BRG:C_BRG + 16], PV[:, l, C_BRG:C_BRG + 16], -1.0, None, ALU.mult, None,
           ["pv_raw"], ["pv_raw"])
        ts("dve", PV[:, l, C_BG:C_BG + 24], PV[:, l, C_BG:C_BG + 24], -1.0, None, ALU.mult, None,
           ["pv_raw"], ["pv_raw"])
    PVK = lambda l: ["pv_raw", ("pv_d", l)]

    prev_batch = [None]

    def conv_batch(name, items, maxn=8):
        for i0 in range(0, len(items), maxn):
            sub = items[i0:i0 + maxn]
            sem = new_sem("cv_%s_%d" % (name, i0))
            tot = 16 * len(sub)
            rd = [prev_batch[0]] if prev_batch[0] is not None else []
            for dst, src, key in sub:
                dma("pool", dst, src, rd, [key], (sem, tot))
            prev_batch[0] = sub[-1][2]

    def kview(ap_, kc):
        return ap_.rearrange("(kc p) c -> p kc c", p=128)

    def convert_layer(l):
        for b5 in range(5):
            items = []
            for g in range(4 * b5, 4 * b5 + 4):
                items.append((WIN[l, g].rearrange("p (kc c) -> p kc c", kc=8),
                              kview(w_in[l][:, g * 256:(g + 1) * 256], 8), ("scr", "win", l, g)))
            conv_batch("win%d_%d" % (l, b5), items)
            if b5 == 0:
                items = []
                for g in range(4):
                    items.append((WPO[l, g].rearrange("p (ic j) -> p ic j", ic=2),
                                  w_pool[l, g].rearrange("(ic p) j -> p ic j", p=128), ("scr", "wpo", l, g)))
                conv_batch("wpo%d" % l, items)
                zsem = new_sem("zf%d" % l)
                for gi in range(4):
                    dma("pool", WBD[l, gi], ZB, [("rsum", 0), prev_batch[0]], [("scr", "wbd", l, gi)], (zsem, 64))
                prev_batch[0] = ("scr", "wbd", l, 3)
                items = []
                for gi in range(4):
                    dstv = WBD[l, gi].rearrange("p (t cc c) -> p t cc c", t=2, cc=2)
                    for t_, wsrc in enumerate((w_rg, w_ig)):
                        for e_ in range(2):
                            h0 = 4 * gi + e_
                            w4 = wsrc[l].rearrange("(c e) i j -> e c i j", e=2)
                            src = w4[e_, 2 * gi:2 * gi + 2].rearrange("c i j -> i c j")
                            dst = dstv[e_ * 64:(e_ + 1) * 64, t_, :, e_ * 64:(e_ + 1) * 64]
                            items.append((dst, src, ("scr", "wbd", l, gi)))
                conv_batch("wbd%d" % l, items)
        for q4 in range(4):
            items = []
            for oc in range(2 * q4, 2 * q4 + 2):
                for b in range(3):
                    dv = WMG[l, oc, b].rearrange("p (kc t c) -> p kc t c", kc=8, t=2)
                    c0 = 5120 + b * 1024 + oc * 128
                    items.append((dv[:, :, 0, :], kview(w_in[l][:, c0:c0 + 128], 8), ("scr", "wmg", l, oc, b)))
                    items.append((dv[:, :, 1, :], kview(w_branch[l, b][:, oc * 128:(oc + 1) * 128], 8),
                                  ("scr", "wmg", l, oc, b)))
            conv_batch("wmg%d_%d" % (l, q4), items)
        items = []
        for g in range(4):
            items.append((WOU[l, g].rearrange("p (kc c) -> p kc c", kc=8),
                          kview(w_out[l][:, g * 256:(g + 1) * 256], 8), ("scr", "wou", l, g)))
        conv_batch("wou%d" % l, items)
        for hb in range(2):
            items = []
            for fc in range(11 * hb, 11 * hb + 11):
                dv = WGU[l, fc].rearrange("p (kc t c) -> p kc t c", kc=8, t=2)
                items.append((dv[:, :, 0, :], kview(w_ffn_gate[l][:, fc * 128:(fc + 1) * 128], 8),
                              ("scr", "wgu", l, fc)))
                items.append((dv[:, :, 1, :], kview(w_ffn_up[l][:, fc * 128:(fc + 1) * 128], 8),
                              ("scr", "wgu", l, fc)))
            conv_batch("wgu%d_%d" % (l, hb), items)
            items = []
            for oc in range(8):
                src = w_ffn_down[l][hb * 1408:(hb + 1) * 1408, oc * 128:(oc + 1) * 128]
                items.append((WDN[l, oc, hb].rearrange("p (kk c) -> p kk c", kk=11),
                              src.rearrange("(kk p) c -> p kk c", p=128), ("scr", "wdn", l, oc, hb)))
            conv_batch("wdn%d_%d" % (l, hb), items)

    for l in range(NL):
        convert_layer(l)

    wsems = [new_sem("ws%d" % i) for i in range(NSLOT)]
    wcount = [0] * NSLOT

    class WStream:
        def __init__(self, log):
            self.log = log
            self.req = []
            self.n = 0
            self.issued = 0

        def _issue(self, m):
            src, key, ne = self.log[m]
            s = m % NSLOT
            wcount[s] += 16
            dma("sp", WS[:, s, 0:ne], src, [key], [("ws", s)], (wsems[s], wcount[s]))

        def get(self, src, key, ne, prev_live=False):
            n = self.n
            self.n += 1
            if self.log is None:
                self.req.append((src, key, ne))
            else:
                depth = NSLOT - 1 if (prev_live or not DEEP) else NSLOT
                while self.issued < min(n + depth, len(self.log)):
                    self._issue(self.issued)
                    self.issued += 1
            s = n % NSLOT
            return s, ("ws", s)

    stsem = [new_sem("stg0"), new_sem("stg1")]
    stcnt = [0, 0]
    stg_i = [0]

    def record_all(ws):
        SCr = Rot([(SC[:, i, :], ("sc", i)) for i in range(NSC)])
        SBr = Rot([(SCB[:, i, :], ("scb", i)) for i in range(NSB)])

        def wsl(s, a, b):
            return WS[:, s, a:b]

        def H2c(kc):
            return MB[:, 3 + kc, :] if kc < 5 else PQ[:, kc - 5, :]

        def H2k(kc):
            return ("MB", 3 + kc) if kc < 5 else ("PQ", kc - 5)

        def ATc(i):
            return UY[:, 16 + i, :] if i < 8 else MB[:, i - 8, :]

        def ATk(i):
            return ("UY", 16 + i) if i < 8 else ("MB", i - 8)

        def rmsnorm(l, j, gbase, dc=None, dk=None):
            dc = dc or (lambda c: HT[:, c, :])
            dk = dk or (lambda c: ("HT", c))
            tok = slice(j * T, (j + 1) * T)
            rsb, rsk_ = SCr.next()
            rs = rsb[:, 0:T]
            for c in range(8):
                act(dc(c), XT[:, c, tok], AF.Square, [("XT", c, j)], [dk(c)])
            ps, pk = PG.next()
            for c in range(8):
                mm(ps[:, :], ONESB, dc(c), c == 0, c == 7, [dk(c), "c_onesb"], [pk])
            act(rs, ps[:, :], AF.Ln, [pk], [rsk_], bias=EPS, scale=1.0 / D)
            act(rs, rs, AF.Exp, [rsk_], [rsk_], scale=-0.5)
            for c in range(8):
                stt(dc(c), XT[:, c, tok], PV[:, l, gbase + c:gbase + c + 1], rs, ALU.mult, ALU.mult,
                    [("XT", c, j), rsk_] + PVK(l), [dk(c)])

        def load_x(s):
            for tb in range(S_ // 128):
                b = stg_i[0] % 2
                stg_i[0] += 1
                stcnt[b] += 16
                dma("sp", STG[:, b, 0:1024], x[s, tb * 128:(tb + 1) * 128, :], (), STGK(b), (stsem[b], stcnt[b]))
                p0, k0 = PG.next()
                p1, k1 = PG.next()
                for cc in range(8):
                    pp, kk = (p0, k0) if cc < 4 else (p1, k1)
                    P.add("pe", lambda e, pp=pp, cc=cc, b=b: e.transpose(
                        pp[:, (cc % 4) * 128:(cc % 4 + 1) * 128], STG[:, b, cc * 128:(cc + 1) * 128], IDENT),
                        STGK(b) + ["c_id"], [kk])
                jt = tb // 4
                cp("act", XT[:, 0:4, tb * 128:(tb + 1) * 128], p0[:, :].rearrange("p (a b) -> p a b", a=4),
                   [k0], [("XT", c, jt) for c in range(4)])
                cp("dve", XT[:, 4:8, tb * 128:(tb + 1) * 128], p1[:, :].rearrange("p (a b) -> p a b", a=4),
                   [k1], [("XT", c, jt) for c in range(4, 8)])

        def store_y(s, j):
            for tb in range(4):
                t0 = j * T + tb * 128
                b = stg_i[0] % 2
                stg_i[0] += 1
                p0, k0 = PG.next()
                p1, k1 = PG.next()
                for cc in range(8):
                    pp, kk = (p0, k0) if cc < 4 else (p1, k1)
                    P.add("pe", lambda e, pp=pp, cc=cc, t0=t0: e.transpose(
                        pp[:, (cc % 4) * 128:(cc % 4 + 1) * 128], XT[:, cc, t0:t0 + 128], IDENT),
                        [("XT", cc, j), "c_id"], [kk])
                cp("act", STG[:, b, 0:512], p0[:, :], [k0], STGK(b))
                cp("dve", STG[:, b, 512:1024], p1[:, :], [k1], STGK(b))
                stcnt[b] += 16
                dma("sp", y[s, t0:t0 + 128, :], STG[:, b, 0:1024], STGK(b), [("yout", s, j, tb)],
                    (stsem[b], stcnt[b]))

        def proj(slot, skey, col0, ncols_slot, rhs_chunks, rkeys):
            ps, pk = PG.next()
            for kc in range(8):
                a = kc * ncols_slot + col0
                mm(ps[:, :], wsl(slot, a, a + 128), rhs_chunks(kc), kc == 0, kc == 7, [skey, rkeys(kc)], [pk])
            return ps, pk

        HTc = lambda kc: HT[:, kc, :]
        HTk = lambda kc: ("HT", kc)

        def pool_branch(l, j):
            for g in range(4):
                w = WINDOWS[g]
                slot, skey = ws.get(WIN[l, g], ("scr", "win", l, g), 2048)
                slot2, skey2 = ws.get(WPO[l, g], ("scr", "wpo", l, g), 512, prev_live=True)
                pts = [SBr.next(), SBr.next()]
                for cc in range(2):
                    c = 2 * g + cc
                    ps, pk = proj(slot, skey, cc * 128, 256, HTc, HTk)
                    ub, uk = SCr.next()
                    cp("act", ub[:, 16:528], ps[:, :], [pk], [uk])
                    if j == 0:
                        memset("dve", ub[:, 1:16], 0.0, [uk])
                    else:
                        cp("dve", ub[:, 1:16], UPH[:, c, :], [("uph", c)], [uk])
                    cp("dve", UPH[:, c, :], ub[:, 513:528], [uk], [("uph", c)])
                    src, sk = ub, uk
                    for lev in range(g + 1):
                        sh = 1 << lev
                        dst, dk = SCr.next()
                        lo = 2 * sh
                        tt("pool", dst[:, lo:528], src[:, lo:528], src[:, lo - sh:528 - sh], ALU.add, [sk], [dk])
                        src, sk = dst, dk
                    stt(pts[cc][0][:, :], src[:, 16:528], 1.0 / w, ub[:, 16:528], ALU.mult, ALU.subtract,
                        [sk, uk], [pts[cc][1]])
                    if j == 0:
                        t16, t16k = SCr.next()
                        tt("dve", t16[:, 0:16], src[:, 16:32], RC[:, g, :], ALU.mult, [sk, ("c_rc", g)], [t16k])
                        tt("dve", pts[cc][0][:, 0:16], t16[:, 0:16], ub[:, 16:32], ALU.subtract, [t16k, uk], [pts[cc][1]])
                slot, skey = slot2, skey2
                for jc in range(2):
                    c = 2 * g + jc
                    ps, pk = PG.next()
                    for ic in range(2):
                        a = ic * 256 + jc * 128
                        mm(ps[:, :], wsl(slot, a, a + 128), pts[ic][0][:, :], ic == 0, ic == 1,
                           [skey, pts[ic][1]], [pk])
                    act(UY[:, c, :], ps[:, :], AF.Copy, [pk] + PVK(l), [("UY", c)],
                        scale=PV[:, l, C_PS + c:C_PS + c + 1])

        def lru_branch(l, j):
            for g in range(4):
                slot, skey = ws.get(WIN[l, 4 + g], ("scr", "win", l, 4 + g), 2048)
                slot2, skey2 = ws.get(WBD[l, g], ("scr", "wbd", l, g), 512, prev_live=True)
                for cc in range(2):
                    c = 2 * g + cc
                    ps, pk = proj(slot, skey, cc * 128, 256, HTc, HTk)
                    ub, uk = SCr.next()
                    cp("act", ub[:, 4:516], ps[:, :], [pk], [uk])
                    if j == 0:
                        memset("dve", ub[:, 0:4], 0.0, [uk])
                    else:
                        cp("dve", ub[:, 0:4], ULH[:, c, :], [("ulh", c)], [uk])
                    cp("dve", ULH[:, c, :], ub[:, 512:516], [uk], [("ulh", c)])
                    xc, xk = SCr.next()
                    cw = lambda k, c=c: PV[:, l, C_CW + 8 * k + c:C_CW + 8 * k + c + 1]
                    ts("dve", xc[:, 0:T], ub[:, 4:516], cw(3), PV[:, l, C_CB + c:C_CB + c + 1], ALU.mult, ALU.add,
                       [uk] + PVK(l), [xk])
                    for k in range(3):
                        stt(xc[:, 0:T], ub[:, 1 + k:1 + k + T], cw(k), xc[:, 0:T], ALU.mult, ALU.add,
                            [uk, xk] + PVK(l), [xk])
                    xb, xbk = SBr.next()
                    cp("dve", xb[:, :], xc[:, 0:T], [xk], [xbk])
                    pr, prk = PG.next()
                    a = 0 * 256 + cc * 128
                    mm(pr[:, :], wsl(slot2, a, a + 128), xb[:, :], True, True, [skey2, xbk], [prk])
                    pi, pik = PG.next()
                    a = 1 * 256 + cc * 128
                    mm(pi[:, :], wsl(slot2, a, a + 128), xb[:, :], True, True, [skey2, xbk], [pik])
                    r_, rk = SCr.next()
                    i_, ik = SCr.next()
                    a_, ak = SCr.next()
                    m_, mk = SCr.next()
                    sigmoid3(r_[:, 0:T], pr[:, :], PV[:, l, C_BRG + c:C_BRG + c + 1], [prk], rk, PVK(l))
                    sigmoid3(i_[:, 0:T], pi[:, :], PV[:, l, C_BIG + c:C_BIG + c + 1], [pik], ik, PVK(l))
                    act(a_[:, 0:T], r_[:, 0:T], AF.Exp, [rk] + PVK(l), [ak], scale=PV[:, l, C_NSP + c:C_NSP + c + 1])
                    act(m_[:, 0:T], r_[:, 0:T], AF.Exp, [rk] + PVK(l), [mk], scale=PV[:, l, C_NSP2 + c:C_NSP2 + c + 1])
                    act(m_[:, 0:T], m_[:, 0:T], AF.Ln, [mk], [mk], bias=1.0, scale=-1.0)
                    act(m_[:, 0:T], m_[:, 0:T], AF.Exp, [mk], [mk], scale=0.5)
                    tt("pool", i_[:, 0:T], i_[:, 0:T], xc[:, 0:T], ALU.mult, [ik, xk], [ik])
                    tt("pool", i_[:, 0:T], i_[:, 0:T], m_[:, 0:T], ALU.mult, [ik, mk], [ik])
                    init = 0.0 if j == 0 else HST[:, c:c + 1]
                    P.add("dve", lambda e, r_=r_, a_=a_, i_=i_, init=init: e.tensor_tensor_scan(
                        r_[:, 0:T], a_[:, 0:T], i_[:, 0:T], init, ALU.mult, ALU.add),
                        [ak, ik, ("hst", c)], [rk], occ=160.0 + T * 2.1)
                    cp("dve", HST[:, c:c + 1], r_[:, T - 1:T], [rk], [("hst", c)])
                    cp("dve", UY[:, 8 + c, :], r_[:, 0:T], [rk], [("UY", 8 + c)])

        def attn_branch(l, j):
            tok0 = j * T
            for hp in range(4):
                for which in range(2):
                    g = 8 + 4 * which + hp
                    slot, skey = ws.get(WIN[l, g], ("scr", "win", l, g), 2048)
                    for hh in range(2):
                        hd = 2 * hp + hh
                        ps, pk = proj(slot, skey, hh * 128, 256, HTc, HTk)
                        sq, sqk = SBr.next()
                        act(sq[:, :], ps[:, :], AF.Square, [pk], [sqk])
                        p2, p2k = PG.next()
                        mm(p2[:, :], ONESB, sq[:, :], True, True, [sqk, "c_onesb"], [p2k])
                        rsb, rsk_ = SCr.next()
                        rs = rsb[:, 0:T]
                        act(rs, p2[:, :], AF.Ln, [p2k], [rsk_], bias=EPS, scale=1.0 / 128)
                        act(rs, rs, AF.Exp, [rsk_], [rsk_], scale=-0.5)
                        if which == 0:
                            stt(PQ[:, hd % 4, :], ps[:, :], PV[:, l, C_QG:C_QG + 1], rs, ALU.mult, ALU.mult,
                                [pk, rsk_] + PVK(l), [("PQ", hd % 4)])
                        else:
                            stt(KT[:, hd, tok0:tok0 + T], ps[:, :], PV[:, l, C_KN:C_KN + 1], rs, ALU.mult, ALU.mult,
                                [pk, rsk_] + PVK(l), [("KT", hd, j)])
                vg = hp
                g = 16 + vg
                slot, skey = ws.get(WIN[l, g], ("scr", "win", l, g), 2048)
                for tb in range(4):
                    ps, pk = PG.next()
                    for kc in range(8):
                        mm(ps[:, 0:256], HT[:, kc, tb * 128:(tb + 1) * 128], wsl(slot, kc * 256, kc * 256 + 256),
                           kc == 0, kc == 7, [skey, ("HT", kc)], [pk])
                    eng = "act" if tb % 2 == 0 else "dve"
                    cp(eng, VC[:, 4 * j + tb, vg * 256:(vg + 1) * 256], ps[:, 0:256], [pk], [("VC", 4 * j + tb, vg)])
                for hh in range(2):
                    scores(l, j, 2 * hp + hh)

        def scores(l, j, hd):
            if True:
                po, pok = PO.next()
                rsum, rsk = RSUM[:, hd % 2, :], ("rsum", hd % 2)
                memset("pool", rsum[:, :], 0.0, [rsk])
                first = True
                for kb in range(4 * j + 3, -1, -1):
                    diag = kb >= 4 * j
                    off = (kb - 4 * j) * 128 if diag else 0
                    N = T - off
                    pz, pzk = PZ.next()
                    kblk = KT[:, hd, kb * 128:(kb + 1) * 128]
                    zr = [("KT", hd, kb // 4), ("PQ", hd % 4)]
                    if not diag:
                        mm(pz[:, 0:N], kblk, PQ[:, hd % 4, off:T], True, True, zr, [pzk])
                    else:
                        mm(pz[:, 0:128], kblk, PQ[:, hd % 4, off:off + 128], True, False, zr, [pzk])
                        mm(pz[:, 0:128], IDENTB, NEGB, False, True, ["c_idb", "c_negb"], [pzk])
                        if N > 128:
                            mm(pz[:, 128:N], kblk, PQ[:, hd % 4, off + 128:T], True, True, zr, [pzk])
                    s_, sk = SCr.next()
                    act(s_[:, 0:N], pz[:, 0:N], AF.Exp, [pzk], [sk], scale=-1.0)
                    act(s_[:, 0:N], s_[:, 0:N], AF.Ln, [sk], [sk], bias=1.0)
                    nl, nlk = SBr.next()
                    tt("dve", nl[:, 0:N], s_[:, 0:N], pz[:, 0:N], ALU.add, [sk, pzk], [nlk])
                    pa, pak = PA.next()
                    mm(pa[:, 0:N], UMAT, nl[:, 0:N], True, first, [nlk, "c_umat"], [pak])
                    if not first:
                        mm(pa[:, 0:N], ONESB, rsum[:, off:T], False, True, [rsk, "c_onesb"], [pak])
                    tt("dve", s_[:, 0:N], s_[:, 0:N], pa[:, 0:N], ALU.add, [sk, pak], [sk])
                    w_, wk = SBr.next()
                    act(w_[:, 0:N], s_[:, 0:N], AF.Exp, [sk], [wk], scale=-1.0)
                    mm(po[:, off:T], VC[:, kb, hd * 128:(hd + 1) * 128], w_[:, 0:N], first, kb == 0,
                       [("VC", kb, hd // 2), wk], [pok], skip=True)
                    if kb > 0:
                        tt("pool", rsum[:, off:T], rsum[:, off:T], nl[:, 0:N], ALU.add, [rsk, nlk], [rsk])
                    first = False
                cp("act", UY[:, 16 + hd, :], po[:, :], [pok], [("UY", 16 + hd)])
                merge01(l, j, hd)

        def gate_branch(l, oc, b, PR):
            slot, skey = ws.get(WMG[l, oc, b], ("scr", "wmg", l, oc, b), 2048)
            pg, pgk = PR.next()
            pb, pbk = PR.next()
            for kc in range(8):
                a = kc * 256
                mm(pg[:, :], wsl(slot, a, a + 128), HT[:, kc, :], kc == 0, kc == 7, [skey, ("HT", kc)], [pgk])
            for kc in range(8):
                a = kc * 256 + 128
                mm(pb[:, :], wsl(slot, a, a + 128), UY[:, 8 * b + kc, :], kc == 0, kc == 7,
                   [skey, ("UY", 8 * b + kc)], [pbk])
            gt, gk = SCr.next()
            sigmoid3(gt[:, 0:T], pg[:, :], PV[:, l, C_BG + 8 * b + oc:C_BG + 8 * b + oc + 1], [pgk], gk, PVK(l))
            tt("dve", gt[:, 0:T], gt[:, 0:T], pb[:, :], ALU.mult, [gk, pbk], [gk])
            return gt, gk

        def merge01(l, j, oc):
            g0, g0k = gate_branch(l, oc, 0, PX)
            g1, g1k = gate_branch(l, oc, 1, PX)
            tt("pool", MB[:, oc, :], g0[:, 0:T], g1[:, 0:T], ALU.add, [g0k, g1k], [("MB", oc)])

        def merge_and_out(l, j):
            tok = slice(j * T, (j + 1) * T)
            for oc in range(8):
                g2, g2k = gate_branch(l, oc, 2, PG)
                tt("pool", MB[:, oc, :], MB[:, oc, :], g2[:, 0:T], ALU.add, [("MB", oc), g2k], [("MB", oc)])
            for g in range(4):
                slot, skey = ws.get(WOU[l, g], ("scr", "wou", l, g), 2048)
                for cc in range(2):
                    oc = 2 * g + cc
                    ps, pk = proj(slot, skey, cc * 128, 256, lambda kc: MB[:, kc, :], lambda kc: ("MB", kc))
                    tt("dve", XT[:, oc, tok], XT[:, oc, tok], ps[:, :], ALU.add, [("XT", oc, j), pk], [("XT", oc, j)])

        def ffn(l, j):
            tok = slice(j * T, (j + 1) * T)
            for p_ in range(2):
                for fi in range(11):
                    fc = 11 * p_ + fi
                    slot, skey = ws.get(WGU[l, fc], ("scr", "wgu", l, fc), 2048)
                    pg, pgk = proj(slot, skey, 0, 256, H2c, H2k)
                    pu, puk = proj(slot, skey, 128, 256, H2c, H2k)
                    sg, sgk = SCr.next()
                    act(sg[:, 0:T], pg[:, :], AF.Exp, [pgk], [sgk], scale=-1.0)
                    act(sg[:, 0:T], sg[:, 0:T], AF.Ln, [sgk], [sgk], bias=1.0)
                    act(sg[:, 0:T], sg[:, 0:T], AF.Exp, [sgk], [sgk], scale=-1.0)
                    tt("dve", sg[:, 0:T], sg[:, 0:T], pg[:, :], ALU.mult, [sgk, pgk], [sgk])
                    tt("dve", ATc(fi), sg[:, 0:T], pu[:, :], ALU.mult, [sgk, puk], [ATk(fi)])
                for oc in range(8):
                    ps, pk = PG.next()
                    slot, skey = ws.get(WDN[l, oc, p_], ("scr", "wdn", l, oc, p_), 1408)
                    for kk in range(11):
                        mm(ps[:, :], wsl(slot, kk * 128, kk * 128 + 128), ATc(kk),
                           kk == 0, kk == 10, [skey, ATk(kk)], [pk])
                    tt("dve", XT[:, oc, tok], XT[:, oc, tok], ps[:, :], ALU.add, [("XT", oc, j), pk], [("XT", oc, j)])

        def early(l, j):
            P.phase = 'rmsnorm'
            rmsnorm(l, j, C_NM)
            P.phase = 'pool_branch'
            pool_branch(l, j)
            P.phase = 'lru_branch'
            lru_branch(l, j)

        def late1(l, j):
            P.phase = 'attn_branch'
            attn_branch(l, j)
            P.phase = 'merge_and_out'
            merge_and_out(l, j)
            P.phase = 'rmsnorm2'
            rmsnorm(l, j, C_NF, H2c, H2k)

        def late2(s, l, j):
            P.phase = 'ffn'
            ffn(l, j)
            if l == NL - 1:
                P.phase = 'store_y'
                store_y(s, j)

        for s in range(NSEQ):
            P.phase = 'load'
            load_x(s)
            stages = [(l, j) for l in range(NL) for j in range(NT)]
            if PIPE:
                early(*stages[0])
                for i, (l, j) in enumerate(stages):
                    late1(l, j)
                    if i + 1 < len(stages):
                        early(*stages[i + 1])
                    late2(s, l, j)
            else:
                for (l, j) in stages:
                    early(l, j)
                    late1(l, j)
                    late2(s, l, j)
        P.add("sp", None, [("yout", s, j, tb) for s in range(NSEQ) for j in range(NT) for tb in range(4)], ())

    saved = (P.ops, P.pages, list(wcount), list(stcnt), stg_i[0])
    P.ops, P.pages = [], {}
    rots = [PG, PZ, PA, PO, PX]
    rsave = [r.i for r in rots]
    ws1 = WStream(None)
    record_all(ws1)
    log = ws1.req
    P.ops, P.pages = saved[0], saved[1]
    wcount[:] = saved[2]
    stcnt[:] = saved[3]
    stg_i[0] = saved[4]
    for r, i in zip(rots, rsave):
        r.i = i
    record_all(WStream(log))

    if speed:
        for o_ in P.ops:
            if o_.eng in speed and not o_.is_dma:
                o_.occ *= speed[o_.eng]
                o_.lat *= speed[o_.eng]
    P.sync_lat = sync_lat
    if FILL:
        fill_rhs = CONB[:, :, :].rearrange("p a b -> p (a b)")
        fill_ps = psum[FILL_BANK]
        P.filler = (FILL[0], FILL[1], FILL[2],
                    lambda e: e.matmul(fill_ps[:, :], ONESB, fill_rhs, start=True, stop=True))
    P.use_blevel = blevel
    P.finalize(new_sem)
    if not emit:
        es.close()
        return nc, P
    with nc.Block() as block:
        @block.tensor
        def _(e):
            P.emit("pe", e)

        @block.scalar
        def _(e):
            P.emit("act", e)

        @block.vector
        def _(e):
            P.emit("dve", e)

        @block.gpsimd
        def _(e):
            P.emit("pool", e)

        @block.sync
        def _(e):
            P.emit("sp", e)
    es.close()
    return nc, P


_CACHE = {}


def kernel(**inputs):
    ncores = 8
    if "nc" not in _CACHE:
        _CACHE["nc"] = build()[0]
    nc = _CACHE["nc"]
    x = np.ascontiguousarray(np.asarray(inputs["x"], dtype=np.float32))
    B = x.shape[0]
    per = B // ncores
    in_maps = []
    for c in range(ncores):
        m = {}
        for k, v in inputs.items():
            if k == "x":
                m["x"] = np.ascontiguousarray(x[c * per:(c + 1) * per])
            else:
                m[k] = np.ascontiguousarray(np.asarray(v, dtype=np.float32))
        in_maps.append(m)
    res = run_bass_kernel_spmd(nc, in_maps, core_ids=list(range(ncores)))
    out = np.concatenate([np.asarray(r["y"]) for r in res.results], axis=0)
    return out.astype(np.float32)
```

```python
import numpy as np
from contextlib import ExitStack
import concourse.bass as bass
import concourse.mybir as mybir
from concourse.bass_utils import run_bass_kernel_spmd

F32 = mybir.dt.float32
BF16 = mybir.dt.bfloat16
AF = mybir.ActivationFunctionType
ALU = mybir.AluOpType

D = 1024
SEQ = 2048
T = 512
DFF = 2816
NFC = 22
EPS = 1e-6
WINDOWS = (2, 4, 8, 16)
EPOCH = 6000


class Op:
    __slots__ = ("eng", "fn", "is_dma", "signal", "deps", "need_sig", "waits", "inc", "occ", "lat", "idx", "tag", "st", "crit", "rdep")

    def __init__(self, eng, fn, is_dma, signal, occ=300.0, lat=None):
        self.occ = occ
        self.lat = occ if lat is None else lat
        self.eng = eng
        self.fn = fn
        self.is_dma = is_dma
        self.signal = signal
        self.deps = {}
        self.need_sig = False
        self.waits = []
        self.inc = 16 if is_dma else 1


class Prog:
    def __init__(self):
        self.ops = []
        self.pages = {}

    def add(self, eng, fn, reads=(), writes=(), dma_signal=None, occ=300.0, lat=None):
        op = Op(eng, fn, dma_signal is not None, dma_signal, occ, lat)
        op.tag = getattr(self, "phase", "")
        op.rdep = None
        deps = op.deps
        is_dma = dma_signal is not None
        for k in reads:
            st = self.pages.get(k)
            if st is None:
                st = self.pages[k] = [{}, {}]
            for w in st[0].values():
                deps[id(w)] = (w, True)
            if fn is not None:
                rkey = id(op) if is_dma else eng
                prev = st[1].get(rkey)
                if prev is not None and prev is not op and id(prev) not in deps:
                    deps[id(prev)] = (prev, None)
                st[1][rkey] = op
        for k in writes:
            st = self.pages.get(k)
            if st is None:
                st = self.pages[k] = [{}, {}]
            for w in st[0].values():
                if id(w) not in deps:
                    deps[id(w)] = (w, False)
            for r in st[1].values():
                if r is not op and (id(r) not in deps or deps[id(r)][1] is None):
                    deps[id(r)] = (r, False)
            st[1] = {}
            wkey = ("dma", id(dma_signal[0])) if is_dma else eng
            st[0][wkey] = op
        self.ops.append(op)
        return op

    def schedule(self, sync_lat=250.0):
        import heapq
        ops = self.ops
        n = len(ops)
        for i, op in enumerate(ops):
            op.idx = i
        succ = [[] for _ in range(n)]
        indeg = [0] * n
        for op in ops:
            for d, raw in op.deps.values():
                succ[d.idx].append(op.idx)
                indeg[op.idx] += 1
        prio = [0.0] * n
        if getattr(self, "use_blevel", False):
            bl = [0.0] * n
            for i in range(n - 1, -1, -1):
                m_ = 0.0
                for j in succ[i]:
                    if bl[j] > m_:
                        m_ = bl[j]
                bl[i] = m_ + ops[i].lat + 100.0
            for i in range(n):
                prio[i] = -bl[i]
        else:
            for i in range(n):
                prio[i] = float(i)
        engs = sorted(set(op.eng for op in ops))
        free = {e: 0.0 for e in engs}
        ready = {e: [] for e in engs}
        avail = {e: [] for e in engs}
        finish = [0.0] * n
        efree = [0.0] * n
        rtime = [0.0] * n
        order = {e: [] for e in engs}
        for op in ops:
            if indeg[op.idx] == 0:
                heapq.heappush(ready[op.eng], (0.0, op.idx))
        done = 0
        while done < n:
            best = None
            for e in engs:
                rq, aq = ready[e], avail[e]
                while rq and rq[0][0] <= free[e]:
                    i_ = heapq.heappop(rq)[1]
                    heapq.heappush(aq, (prio[i_], i_))
                if aq:
                    cand = (free[e], aq[0][1], e, True)
                elif rq:
                    cand = (rq[0][0], rq[0][1], e, False)
                else:
                    continue
                if best is None or cand < best:
                    best = cand
            st, i, e, from_avail = best
            if from_avail:
                heapq.heappop(avail[e])
            else:
                heapq.heappop(ready[e])
            op = ops[i]
            op.st = st
            prev_on_eng = order[e][-1] if order[e] else None
            if prev_on_eng is not None and free[e] >= rtime[i]:
                op.crit = ("res", prev_on_eng)
            else:
                op.crit = ("dep", getattr(op, "rdep", None))
            free[e] = st + op.occ
            efree[i] = st + op.occ
            finish[i] = st + op.lat
            order[e].append(op)
            done += 1
            for j in succ[i]:
                o2 = ops[j]
                same = (o2.eng == op.eng and not op.is_dma and not o2.is_dma)
                if same and (op.eng == "pe" or o2.deps[id(op)][1] is None):
                    t = efree[i]
                elif same:
                    t = finish[i] + 80.0
                else:
                    t = finish[i] + sync_lat
                if t > rtime[j]:
                    rtime[j] = t
                    o2.rdep = op
                indeg[j] -= 1
                if indeg[j] == 0:
                    heapq.heappush(ready[o2.eng], (rtime[j], j))
        nfill = 0
        ff = getattr(self, "filler", None)
        if ff is not None:
            frac, cap, mingap, fn_ = ff
            newpe = []
            prev_end = 0.0
            for op in order["pe"]:
                gap = op.st - prev_end
                if gap > mingap and op.st > 60000.0:
                    k = min(int(frac * gap / 220.0), cap)
                    for _ in range(k):
                        fo = Op("pe", fn_, False, None, 216.0, 216.0)
                        fo.tag = "filler"
                        fo.st = prev_end
                        newpe.append(fo)
                        nfill += 1
                newpe.append(op)
                prev_end = op.st + op.occ
            order["pe"] = newpe
        self.nfill = nfill
        self.order = order
        self.makespan = max(finish) if n else 0.0
        return order

    def finalize(self, new_sem, reorder=True):
        if reorder:
            self.schedule(getattr(self, 'sync_lat', 250.0))
            newops = []
            for e in self.order:
                newops.extend(self.order[e])
            self.ops = newops
        for op in self.ops:
            for d, raw in op.deps.values():
                if raw is None:
                    continue
                if d.is_dma and op.is_dma and d.signal[0] is op.signal[0] and d.signal[1] == op.signal[1]:
                    continue
                if d.eng == op.eng and not d.is_dma and not op.is_dma:
                    if op.eng == "pe":
                        continue
                d.need_sig = True
                op.waits.append(d)
        cnt = {}
        sems = {}
        for op in self.ops:
            if op.is_dma or not op.need_sig:
                continue
            n = cnt.get(op.eng, 0)
            cnt[op.eng] = n + 1
            key = (op.eng, n // EPOCH)
            if key not in sems:
                sems[key] = new_sem("s_%s_%d" % key)
            op.signal = (sems[key], n % EPOCH + 1)

    def emit(self, eng_name, e):
        waited = {}
        for op in self.ops:
            if op.eng != eng_name:
                continue
            for d in op.waits:
                sem, v = d.signal
                if waited.get(id(sem), 0) < v:
                    e.wait_ge(sem, v)
                    waited[id(sem)] = v
            if op.fn is None:
                continue
            ins = op.fn(e)
            if op.is_dma or op.need_sig:
                ins.then_inc(op.signal[0], op.inc)


class Rot:
    def __init__(self, items):
        self.items = items
        self.i = 0

    def next(self):
        it = self.items[self.i % len(self.items)]
        self.i += 1
        return it


def build(NSEQ=2, NL=2, NT=4, NSLOT=3, NSC=7, NSB=4, whatif=False, blevel=False, PIPE=True, DEEP=True, ESTRIDE=4, FILL=(0.6, 12, 500.0), FILL_BANK=5, sync_lat=320.0, PSCFG=([0, 1, 2, 3, 6, 7], [0, 1], [2, 3], [4]), speed=None, emit=True, PE_A=8.0, PE_B=0.417):
    nc = bass.Bass("TRN2", target_bir_lowering=False)
    S_ = NT * T
    dram = {}

    def din(name, shape):
        dram[name] = nc.dram_tensor(name, list(shape), F32, kind="ExternalInput").ap()
        return dram[name]

    x = din("x", [NSEQ, S_, D])
    norm_mix = din("norm_mix", [NL, D])
    w_in = din("w_in", [NL, D, 8192])
    b_gate = din("b_gate", [NL, 3 * D])
    w_pool = din("w_pool", [NL, 4, 256, 256])
    pool_scale = din("pool_scale", [NL, D])
    conv_w = din("conv_w", [NL, 4, D])
    conv_b = din("conv_b", [NL, D])
    w_rg = din("w_rg", [NL, 16, 64, 64])
    b_rg = din("b_rg", [NL, D])
    w_ig = din("w_ig", [NL, 16, 64, 64])
    b_ig = din("b_ig", [NL, D])
    lru_lambda = din("lru_lambda", [NL, D])
    q_norm = din("q_norm", [NL, 128])
    k_norm = din("k_norm", [NL, 128])
    w_branch = din("w_branch", [NL, 3, D, D])
    w_out = din("w_out", [NL, D, D])
    norm_ffn = din("norm_ffn", [NL, D])
    w_ffn_gate = din("w_ffn_gate", [NL, D, DFF])
    w_ffn_up = din("w_ffn_up", [NL, D, DFF])
    w_ffn_down = din("w_ffn_down", [NL, DFF, D])
    y = nc.dram_tensor("y", [NSEQ, S_, D], F32, kind="ExternalOutput").ap()
    dbg_out = {}

    def scr(name, shape):
        return nc.dram_tensor(name, list(shape), BF16, kind="Internal").ap()

    WIN = scr("s_win", [NL, 20, 128, 2048])
    WPO = scr("s_wpo", [NL, 4, 128, 512])
    WBD = scr("s_wbd", [NL, 4, 128, 512])
    WMG = scr("s_wmg", [NL, 8, 3, 128, 2048])
    WOU = scr("s_wou", [NL, 4, 128, 2048])
    WGU = scr("s_wgu", [NL, NFC, 128, 2048])
    WDN = scr("s_wdn", [NL, 8, 2, 128, 1408])

    P = Prog()
    es = ExitStack()

    def sb(name, shape, dt):
        return es.enter_context(nc.sbuf_tensor(name, list(shape), dt))

    sem_list = []

    def new_sem(name):
        s = es.enter_context(nc.semaphore(name))
        sem_list.append(s)
        return s

    XT = sb("XT", [128, 8, SEQ], BF16 if whatif else F32)
    KT = sb("KT", [128, 8, SEQ], BF16)
    VC = sb("VC", [128, 16, 1024], BF16)
    HT = sb("HT", [128, 8, T], BF16)
    UY = sb("UY", [128, 24, T], BF16)
    PQ = sb("PQ", [128, 4, T], BF16)
    MB = sb("MB", [128, 8, T], BF16)
    WS = sb("WS", [128, NSLOT, 2048], BF16)
    PV = sb("PV", [128, NL, 132], F32)
    IDT = sb("IDT", [128, 128], F32)
    CONB = sb("CONB", [128, 4, 128], BF16)
    RSUM = sb("RSUM", [128, 1, T], BF16)
    WSM = sb("WSM", [128, 512], BF16)
    ZB = RSUM[:, 0, :]
    RC = sb("RC", [128, 4, 16], F32)
    UPH = sb("UPH", [128, 8, 15], F32)
    ULH = sb("ULH", [128, 8, 4], F32)
    HST = sb("HST", [128, 8], F32)
    SCF = sb("SC", [128, NSC * 528], F32)

    class _SC:
        def __getitem__(self, idx):
            p_, i_, c_ = idx
            assert isinstance(i_, int)
            return SCF[p_, i_ * 528:(i_ + 1) * 528][:, c_]
    SC = _SC()

    class _STG:
        def __getitem__(self, idx):
            p_, b_, c_ = idx
            return SCF[p_, b_ * 1056:b_ * 1056 + 1024][:, c_]
    STG = _STG()
    STGK = lambda b: [("sc", 2 * b), ("sc", 2 * b + 1)]
    SCB = sb("SCB", [128, NSB, T], BF16)

    psum = [es.enter_context(nc.psum_tensor("ps%d" % i, [128, 512], F32)) for i in range(8)]

    def psrot(ids):
        return Rot([(psum[i], ("ps", i)) for i in ids])

    PG = psrot(PSCFG[0])
    PZ = psrot(PSCFG[1])
    PA = psrot(PSCFG[2])
    PO = psrot(PSCFG[3])
    PX = psrot([6, 7])

    IOT = SC[:, 3, 0:16]
    ONESF = SC[:, 0, 0:128]
    IDENT = IDT[:, :]
    MASKF = SC[:, 1, 0:128]
    UF = SC[:, 2, 0:128]
    ONESB = CONB[:, 0, :]
    NEGB = CONB[:, 1, :]
    UMAT = CONB[:, 2, :]
    IDENTB = CONB[:, 3, :]

    def fsz(ap_):
        n_ = 1
        for d_ in ap_.shape[1:]:
            n_ *= int(d_)
        return n_

    def mm(out, lhsT, rhs, start, stop, reads, writes, skip=False):
        n_ = fsz(rhs)
        P.add("pe", lambda e: e.matmul(out, lhsT, rhs, start=start, stop=stop, skip_group_check=skip),
              reads, writes, occ=PE_A + max(n_, 64) * PE_B, lat=200.0 + n_ * PE_B)

    def act(out, in_, func, reads, writes, bias=None, scale=None):
        kw = {}
        if bias is not None:
            kw["bias"] = bias
        if scale is not None:
            kw["scale"] = scale
        P.add("act", lambda e: e.activation(out, in_, func, **kw), reads, writes, occ=150.0 + fsz(out) * 0.75)

    def sigmoid3(out, in_, negb, rin, wkey, extra):
        act(out, in_, AF.Exp, rin + extra, [wkey], bias=negb, scale=-1.0)
        act(out, out, AF.Ln, [wkey], [wkey], bias=1.0)
        act(out, out, AF.Exp, [wkey], [wkey], scale=-1.0)

    def tt(eng, out, in0, in1, op, reads, writes):
        P.add(eng, lambda e: e.tensor_tensor(out, in0, in1, op), reads, writes,
              occ=(100.0 + fsz(out) * 0.95) if eng == "dve" else (200.0 + fsz(out) * 1.66))

    def stt(out, in0, scalar, in1, op0, op1, reads, writes):
        P.add("dve", lambda e: e.scalar_tensor_tensor(out, in0, scalar, in1, op0, op1), reads, writes,
              occ=160.0 + fsz(out) * 1.05)

    def ts(eng, out, in0, s1, s2, op0, op1, reads, writes):
        oc_ = (160.0 + fsz(out) * 0.6) if eng == "dve" else (250.0 + fsz(out) * 1.7)
        if op1 is None:
            P.add(eng, lambda e: e.tensor_scalar(out, in0, s1, None, op0), reads, writes, occ=oc_)
        else:
            P.add(eng, lambda e: e.tensor_scalar(out, in0, s1, s2, op0, op1), reads, writes, occ=oc_)

    def cp(eng, out, in_, reads, writes):
        if eng == "act":
            P.add("act", lambda e: e.copy(out, in_), reads, writes, occ=260.0 + fsz(out) * 0.85)
        else:
            P.add(eng, lambda e: e.tensor_copy(out, in_), reads, writes,
                  occ=(160.0 + fsz(out) * 0.8) if eng == "dve" else (250.0 + fsz(out) * 1.5))

    def memset(eng, ap, val, writes):
        P.add(eng, lambda e: e.memset(ap, val), (), writes, occ=160.0 + fsz(ap) * 1.0)

    def dma(q, out, in_, reads, writes, signal, **kw):
        nel = 1
        for d_ in out.shape:
            nel *= int(d_)
        P.add(q, lambda e: e.dma_start(out=out, in_=in_, **kw), reads, writes, dma_signal=signal,
              occ=(1500.0 if q == "pool" else 120.0), lat=2200.0 + nel * 2 * 0.006)

    CK = [("sc", 0)]
    memset("pool", ONESF, 1.0, CK)
    P.add("pool", lambda e: e.affine_select(IDENT, ONESF, [[1, 128]], ALU.is_equal, 0.0, base=0,
                                            channel_multiplier=-1), CK, ["c_id"])
    P.add("pool", lambda e: e.affine_select(MASKF, ONESF, [[1, 128]], ALU.is_gt, 0.0, base=0,
                                            channel_multiplier=-1), CK, [("sc", 1)])
    P.add("pool", lambda e: e.affine_select(UF, ONESF, [[-1, 128]], ALU.is_gt, 0.0, base=0,
                                            channel_multiplier=1), CK, [("sc", 2)])
    P.add("pool", lambda e: e.iota(IOT, [[1, 16]], base=1, channel_multiplier=0,
                                   allow_small_or_imprecise_dtypes=True), (), [("sc", 3)])
    memset("pool", ZB, 0.0, [("rsum", 0)])
    cp("dve", ONESB, ONESF, CK, ["c_onesb"])
    ts("dve", NEGB, MASKF, 28.0, -28.0, ALU.mult, ALU.add, [("sc", 1)], ["c_negb"])
    cp("dve", IDENTB, IDENT, ["c_id"], ["c_idb"])
    cp("dve", UMAT, UF, [("sc", 2)], ["c_umat"])
    for g, w in enumerate(WINDOWS):
        ts("dve", RC[:, g, :], IOT, float(w), None, ALU.min, None, [("sc", 3)], [("c_rc", g)])
        P.add("dve", lambda e, g=g: e.reciprocal(RC[:, g, :], RC[:, g, :]), [("c_rc", g)], [("c_rc", g)])

    C_NM, C_PS, C_CW, C_CB, C_BRG, C_BIG, C_LAM, C_NF, C_BG, C_QN, C_KN, C_NSP, C_NSP2, C_QG, C_TMP = (
        0, 8, 16, 48, 56, 64, 72, 80, 88, 112, 113, 114, 122, 130, 72)
    psem = new_sem("psem")
    pl = []
    for l in range(NL):
        def vec8(src):
            return src.rearrange("(c p) -> p c", p=128)
        pl.append((PV[:, l, C_NM:C_NM + 8], vec8(norm_mix[l])))
        pl.append((PV[:, l, C_PS:C_PS + 8], vec8(pool_scale[l])))
        for k in range(4):
            pl.append((PV[:, l, C_CW + 8 * k:C_CW + 8 * k + 8], vec8(conv_w[l, k])))
        pl.append((PV[:, l, C_CB:C_CB + 8], vec8(conv_b[l])))
        pl.append((PV[:, l, C_BRG:C_BRG + 8], vec8(b_rg[l])))
        pl.append((PV[:, l, C_BIG:C_BIG + 8], vec8(b_ig[l])))
        pl.append((PV[:, l, C_LAM:C_LAM + 8], vec8(lru_lambda[l])))
        pl.append((PV[:, l, C_NF:C_NF + 8], vec8(norm_ffn[l])))
        pl.append((PV[:, l, C_BG:C_BG + 24], b_gate[l].rearrange("(c p) -> p c", p=128)))
        pl.append((PV[:, l, C_QN:C_QN + 1], q_norm[l].rearrange("(p o) -> p o", o=1)))
        pl.append((PV[:, l, C_KN:C_KN + 1], k_norm[l].rearrange("(p o) -> p o", o=1)))
    ptotal = 16 * len(pl)
    for o_, i_ in pl:
        dma("sp", o_, i_, (), ["pv_raw"], (psem, ptotal), allow_slow_non_contiguous=True)
    for l in range(NL):
        act(PV[:, l, C_TMP:C_TMP + 8], PV[:, l, C_LAM:C_LAM + 8], AF.Exp, ["pv_raw"], [("pv_t", l)], scale=-1.0)
        act(PV[:, l, C_TMP:C_TMP + 8], PV[:, l, C_TMP:C_TMP + 8], AF.Ln, [("pv_t", l)], [("pv_t", l)], bias=1.0)
        ts("dve", PV[:, l, C_NSP:C_NSP + 8], PV[:, l, C_TMP:C_TMP + 8], -8.0, None, ALU.mult, None,
           [("pv_t", l)], [("pv_d", l)])
        ts("dve", PV[:, l, C_NSP2:C_NSP2 + 8], PV[:, l, C_TMP:C_TMP + 8], -16.0, None, ALU.mult, None,
           [("pv_t", l)], [("pv_d", l)])
        ts("dve", PV[:, l, C_QG:C_QG + 1], PV[:, l, C_QN:C_QN + 1], 128.0 ** -0.5, None, ALU.mult, None,
           ["pv_raw"], [("pv_d", l)])
        ts("dve", PV[:, l, C_BRG:C_BRG + 16], PV[:, l, C_BRG:C_BRG + 16], -1.0, None, ALU.mult, None,
           ["pv_raw"], ["pv_raw"])
        ts("dve", PV[:, l, C_BG:C_BG + 24], PV[:, l, C_BG:C_BG + 24], -1.0, None, ALU.mult, None,
           ["pv_raw"], ["pv_raw"])
    PVK = lambda l: ["pv_raw", ("pv_d", l)]

    prev_batch = [None]

    def conv_batch(name, items, maxn=8):
        for i0 in range(0, len(items), maxn):
            sub = items[i0:i0 + maxn]
            sem = new_sem("cv_%s_%d" % (name, i0))
            tot = 16 * len(sub)
            rd = [prev_batch[0]] if prev_batch[0] is not None else []
            for dst, src, key in sub:
                dma("pool", dst, src, rd, [key], (sem, tot))
            prev_batch[0] = sub[-1][2]

    def kview(ap_, kc):
        return ap_.rearrange("(kc p) c -> p kc c", p=128)

    def convert_layer(l):
        for b5 in range(5):
            items = []
            for g in range(4 * b5, 4 * b5 + 4):
                items.append((WIN[l, g].rearrange("p (kc c) -> p kc c", kc=8),
                              kview(w_in[l][:, g * 256:(g + 1) * 256], 8), ("scr", "win", l, g)))
            conv_batch("win%d_%d" % (l, b5), items)
            if b5 == 0:
                items = []
                for g in range(4):
                    items.append((WPO[l, g].rearrange("p (ic j) -> p ic j", ic=2),
                                  w_pool[l, g].rearrange("(ic p) j -> p ic j", p=128), ("scr", "wpo", l, g)))
                conv_batch("wpo%d" % l, items)
                zsem = new_sem("zf%d" % l)
                for gi in range(4):
                    dma("pool", WBD[l, gi], ZB, [("rsum", 0), prev_batch[0]], [("scr", "wbd", l, gi)], (zsem, 64))
                prev_batch[0] = ("scr", "wbd", l, 3)
                items = []
                for gi in range(4):
                    dstv = WBD[l, gi].rearrange("p (t cc c) -> p t cc c", t=2, cc=2)
                    for t_, wsrc in enumerate((w_rg, w_ig)):
                        for e_ in range(2):
                            h0 = 4 * gi + e_
                            w4 = wsrc[l].rearrange("(c e) i j -> e c i j", e=2)
                            src = w4[e_, 2 * gi:2 * gi + 2].rearrange("c i j -> i c j")
                            dst = dstv[e_ * 64:(e_ + 1) * 64, t_, :, e_ * 64:(e_ + 1) * 64]
                            items.append((dst, src, ("scr", "wbd", l, gi)))
                conv_batch("wbd%d" % l, items)
        for q4 in range(4):
            items = []
            for oc in range(2 * q4, 2 * q4 + 2):
                for b in range(3):
                    dv = WMG[l, oc, b].rearrange("p (kc t c) -> p kc t c", kc=8, t=2)
                    c0 = 5120 + b * 1024 + oc * 128
                    items.append((dv[:, :, 0, :], kview(w_in[l][:, c0:c0 + 128], 8), ("scr", "wmg", l, oc, b)))
                    items.append((dv[:, :, 1, :], kview(w_branch[l, b][:, oc * 128:(oc + 1) * 128], 8),
                                  ("scr", "wmg", l, oc, b)))
            conv_batch("wmg%d_%d" % (l, q4), items)
        items = []
        for g in range(4):
            items.append((WOU[l, g].rearrange("p (kc c) -> p kc c", kc=8),
                          kview(w_out[l][:, g * 256:(g + 1) * 256], 8), ("scr", "wou", l, g)))
        conv_batch("wou%d" % l, items)
        for hb in range(2):
            items = []
            for fc in range(11 * hb, 11 * hb + 11):
                dv = WGU[l, fc].rearrange("p (kc t c) -> p kc t c", kc=8, t=2)
                items.append((dv[:, :, 0, :], kview(w_ffn_gate[l][:, fc * 128:(fc + 1) * 128], 8),
                              ("scr", "wgu", l, fc)))
                items.append((dv[:, :, 1, :], kview(w_ffn_up[l][:, fc * 128:(fc + 1) * 128], 8),
                              ("scr", "wgu", l, fc)))
            conv_batch("wgu%d_%d" % (l, hb), items)
            items = []
            for oc in range(8):
                src = w_ffn_down[l][hb * 1408:(hb + 1) * 1408, oc * 128:(oc + 1) * 128]
                items.append((WDN[l, oc, hb].rearrange("p (kk c) -> p kk c", kk=11),
                              src.rearrange("(kk p) c -> p kk c", p=128), ("scr", "wdn", l, oc, hb)))
            conv_batch("wdn%d_%d" % (l, hb), items)

    for l in range(NL):
        convert_layer(l)

    wsems = [new_sem("ws%d" % i) for i in range(NSLOT)]
    wcount = [0] * NSLOT

    class WStream:
        def __init__(self, log):
            self.log = log
            self.req = []
            self.n = 0
            self.issued = 0

        def _issue(self, m):
            src, key, ne = self.log[m]
            s = m % NSLOT
            wcount[s] += 16
            dma("sp", WS[:, s, 0:ne], src, [key], [("ws", s)], (wsems[s], wcount[s]))

        def get(self, src, key, ne, prev_live=False):
            n = self.n
            self.n += 1
            if self.log is None:
                self.req.append((src, key, ne))
            else:
                depth = NSLOT - 1 if (prev_live or not DEEP) else NSLOT
                while self.issued < min(n + depth, len(self.log)):
                    self._issue(self.issued)
                    self.issued += 1
            s = n % NSLOT
            return s, ("ws", s)

    wsm_sem = new_sem("wsm")
    wsmcnt = [0]

    def load_wsm(src, key):
        wsmcnt[0] += 16
        dma("sp", WSM[:, 0:512], src, [key], ["wsm"], (wsm_sem, wsmcnt[0]))

    stsem = [new_sem("stg0"), new_sem("stg1")]
    stcnt = [0, 0]
    stg_i = [0]

    def record_all(ws):
        SCr = Rot([(SC[:, i, :], ("sc", i)) for i in range(NSC)])
        SBr = Rot([(SCB[:, i, :], ("scb", i)) for i in range(NSB)])

        def wsl(s, a, b):
            return WS[:, s, a:b]

        def H2c(kc):
            return MB[:, 3 + kc, :] if kc < 5 else PQ[:, kc - 5, :]

        def H2k(kc):
            return ("MB", 3 + kc) if kc < 5 else ("PQ", kc - 5)

        def ATc(i):
            return UY[:, 16 + i, :] if i < 8 else MB[:, i - 8, :]

        def ATk(i):
            return ("UY", 16 + i) if i < 8 else ("MB", i - 8)

        def rmsnorm(l, j, gbase, dc=None, dk=None):
            dc = dc or (lambda c: HT[:, c, :])
            dk = dk or (lambda c: ("HT", c))
            tok = slice(j * T, (j + 1) * T)
            rsb, rsk_ = SCr.next()
            rs = rsb[:, 0:T]
            for c in range(8):
                act(dc(c), XT[:, c, tok], AF.Square, [("XT", c, j)], [dk(c)])
            ps, pk = PG.next()
            for c in range(8):
                mm(ps[:, :], ONESB, dc(c), c == 0, c == 7, [dk(c), "c_onesb"], [pk])
            act(rs, ps[:, :], AF.Ln, [pk], [rsk_], bias=EPS, scale=1.0 / D)
            act(rs, rs, AF.Exp, [rsk_], [rsk_], scale=-0.5)
            for c in range(8):
                stt(dc(c), XT[:, c, tok], PV[:, l, gbase + c:gbase + c + 1], rs, ALU.mult, ALU.mult,
                    [("XT", c, j), rsk_] + PVK(l), [dk(c)])

        def load_x(s):
            for tb in range(S_ // 128):
                b = stg_i[0] % 2
                stg_i[0] += 1
                stcnt[b] += 16
                dma("sp", STG[:, b, 0:1024], x[s, tb * 128:(tb + 1) * 128, :], (), STGK(b), (stsem[b], stcnt[b]))
                p0, k0 = PG.next()
                p1, k1 = PG.next()
                for cc in range(8):
                    pp, kk = (p0, k0) if cc < 4 else (p1, k1)
                    P.add("pe", lambda e, pp=pp, cc=cc, b=b: e.transpose(
                        pp[:, (cc % 4) * 128:(cc % 4 + 1) * 128], STG[:, b, cc * 128:(cc + 1) * 128], IDENT),
                        STGK(b) + ["c_id"], [kk])
                jt = tb // 4
                cp("act", XT[:, 0:4, tb * 128:(tb + 1) * 128], p0[:, :].rearrange("p (a b) -> p a b", a=4),
                   [k0], [("XT", c, jt) for c in range(4)])
                cp("dve", XT[:, 4:8, tb * 128:(tb + 1) * 128], p1[:, :].rearrange("p (a b) -> p a b", a=4),
                   [k1], [("XT", c, jt) for c in range(4, 8)])

        def store_y(s, j):
            for tb in range(4):
                t0 = j * T + tb * 128
                b = stg_i[0] % 2
                stg_i[0] += 1
                p0, k0 = PG.next()
                p1, k1 = PG.next()
                for cc in range(8):
                    pp, kk = (p0, k0) if cc < 4 else (p1, k1)
                    P.add("pe", lambda e, pp=pp, cc=cc, t0=t0: e.transpose(
                        pp[:, (cc % 4) * 128:(cc % 4 + 1) * 128], XT[:, cc, t0:t0 + 128], IDENT),
                        [("XT", cc, j), "c_id"], [kk])
                cp("act", STG[:, b, 0:512], p0[:, :], [k0], STGK(b))
                cp("dve", STG[:, b, 512:1024], p1[:, :], [k1], STGK(b))
                stcnt[b] += 16
                dma("sp", y[s, t0:t0 + 128, :], STG[:, b, 0:1024], STGK(b), [("yout", s, j, tb)],
                    (stsem[b], stcnt[b]))

        def proj(slot, skey, col0, ncols_slot, rhs_chunks, rkeys):
            ps, pk = PG.next()
            for kc in range(8):
                a = kc * ncols_slot + col0
                mm(ps[:, :], wsl(slot, a, a + 128), rhs_chunks(kc), kc == 0, kc == 7, [skey, rkeys(kc)], [pk])
            return ps, pk

        HTc = lambda kc: HT[:, kc, :]
        HTk = lambda kc: ("HT", kc)

        def pool_branch(l, j):
            for g in range(4):
                pool_unit(l, j, g)

        def pool_unit(l, j, g):
            if True:
                w = WINDOWS[g]
                slot, skey = ws.get(WIN[l, g], ("scr", "win", l, g), 2048)
                load_wsm(WPO[l, g], ("scr", "wpo", l, g))
                pts = [SBr.next(), SBr.next()]
                for cc in range(2):
                    c = 2 * g + cc
                    ps, pk = proj(slot, skey, cc * 128, 256, HTc, HTk)
                    ub, uk = SCr.next()
                    cp("act", ub[:, 16:528], ps[:, :], [pk], [uk])
                    if j == 0:
                        memset("dve", ub[:, 1:16], 0.0, [uk])
                    else:
                        cp("dve", ub[:, 1:16], UPH[:, c, :], [("uph", c)], [uk])
                    cp("dve", UPH[:, c, :], ub[:, 513:528], [uk], [("uph", c)])
                    src, sk = ub, uk
                    for lev in range(g + 1):
                        sh = 1 << lev
                        dst, dk = SCr.next()
                        lo = 2 * sh
                        tt("pool", dst[:, lo:528], src[:, lo:528], src[:, lo - sh:528 - sh], ALU.add, [sk], [dk])
                        src, sk = dst, dk
                    stt(pts[cc][0][:, :], src[:, 16:528], 1.0 / w, ub[:, 16:528], ALU.mult, ALU.subtract,
                        [sk, uk], [pts[cc][1]])
                    if j == 0:
                        t16, t16k = SCr.next()
                        tt("dve", t16[:, 0:16], src[:, 16:32], RC[:, g, :], ALU.mult, [sk, ("c_rc", g)], [t16k])
                        tt("dve", pts[cc][0][:, 0:16], t16[:, 0:16], ub[:, 16:32], ALU.subtract, [t16k, uk], [pts[cc][1]])
                for jc in range(2):
                    c = 2 * g + jc
                    ps, pk = PG.next()
                    for ic in range(2):
                        a = ic * 256 + jc * 128
                        mm(ps[:, :], WSM[:, a:a + 128], pts[ic][0][:, :], ic == 0, ic == 1,
                           ["wsm", pts[ic][1]], [pk])
                    act(UY[:, c, :], ps[:, :], AF.Copy, [pk] + PVK(l), [("UY", c)],
                        scale=PV[:, l, C_PS + c:C_PS + c + 1])

        def lru_branch(l, j):
            for g in range(4):
                lru_unit(l, j, g)

        def lru_unit(l, j, g):
            if True:
                slot, skey = ws.get(WIN[l, 4 + g], ("scr", "win", l, 4 + g), 2048)
                load_wsm(WBD[l, g], ("scr", "wbd", l, g))
                for cc in range(2):
                    c = 2 * g + cc
                    ps, pk = proj(slot, skey, cc * 128, 256, HTc, HTk)
                    ub, uk = SCr.next()
                    cp("act", ub[:, 4:516], ps[:, :], [pk], [uk])
                    if j == 0:
                        memset("dve", ub[:, 0:4], 0.0, [uk])
                    else:
                        cp("dve", ub[:, 0:4], ULH[:, c, :], [("ulh", c)], [uk])
                    cp("dve", ULH[:, c, :], ub[:, 512:516], [uk], [("ulh", c)])
                    xc, xk = SCr.next()
                    cw = lambda k, c=c: PV[:, l, C_CW + 8 * k + c:C_CW + 8 * k + c + 1]
                    ts("dve", xc[:, 0:T], ub[:, 4:516], cw(3), PV[:, l, C_CB + c:C_CB + c + 1], ALU.mult, ALU.add,
                       [uk] + PVK(l), [xk])
                    for k in range(3):
                        stt(xc[:, 0:T], ub[:, 1 + k:1 + k + T], cw(k), xc[:, 0:T], ALU.mult, ALU.add,
                            [uk, xk] + PVK(l), [xk])
                    xb, xbk = SBr.next()
                    cp("dve", xb[:, :], xc[:, 0:T], [xk], [xbk])
                    pr, prk = PG.next()
                    a = 0 * 256 + cc * 128
                    mm(pr[:, :], WSM[:, a:a + 128], xb[:, :], True, True, ["wsm", xbk], [prk])
                    pi, pik = PG.next()
                    a = 1 * 256 + cc * 128
                    mm(pi[:, :], WSM[:, a:a + 128], xb[:, :], True, True, ["wsm", xbk], [pik])
                    r_, rk = SCr.next()
                    i_, ik = SCr.next()
                    a_, ak = SCr.next()
                    m_, mk = SCr.next()
                    sigmoid3(r_[:, 0:T], pr[:, :], PV[:, l, C_BRG + c:C_BRG + c + 1], [prk], rk, PVK(l))
                    sigmoid3(i_[:, 0:T], pi[:, :], PV[:, l, C_BIG + c:C_BIG + c + 1], [pik], ik, PVK(l))
                    act(a_[:, 0:T], r_[:, 0:T], AF.Exp, [rk] + PVK(l), [ak], scale=PV[:, l, C_NSP + c:C_NSP + c + 1])
                    act(m_[:, 0:T], r_[:, 0:T], AF.Exp, [rk] + PVK(l), [mk], scale=PV[:, l, C_NSP2 + c:C_NSP2 + c + 1])
                    act(m_[:, 0:T], m_[:, 0:T], AF.Ln, [mk], [mk], bias=1.0, scale=-1.0)
                    act(m_[:, 0:T], m_[:, 0:T], AF.Exp, [mk], [mk], scale=0.5)
                    tt("pool", i_[:, 0:T], i_[:, 0:T], xc[:, 0:T], ALU.mult, [ik, xk], [ik])
                    tt("pool", i_[:, 0:T], i_[:, 0:T], m_[:, 0:T], ALU.mult, [ik, mk], [ik])
                    init = 0.0 if j == 0 else HST[:, c:c + 1]
                    P.add("dve", lambda e, r_=r_, a_=a_, i_=i_, init=init: e.tensor_tensor_scan(
                        r_[:, 0:T], a_[:, 0:T], i_[:, 0:T], init, ALU.mult, ALU.add),
                        [ak, ik, ("hst", c)], [rk], occ=160.0 + T * 2.1)
                    cp("dve", HST[:, c:c + 1], r_[:, T - 1:T], [rk], [("hst", c)])
                    cp("dve", UY[:, 8 + c, :], r_[:, 0:T], [rk], [("UY", 8 + c)])

        def attn_branch(l, j):
            tok0 = j * T
            for hp in range(4):
                for which in range(2):
                    g = 8 + 4 * which + hp
                    slot, skey = ws.get(WIN[l, g], ("scr", "win", l, g), 2048)
                    for hh in range(2):
                        hd = 2 * hp + hh
                        ps, pk = proj(slot, skey, hh * 128, 256, HTc, HTk)
                        sq, sqk = SBr.next()
                        act(sq[:, :], ps[:, :], AF.Square, [pk], [sqk])
                        p2, p2k = PG.next()
                        mm(p2[:, :], ONESB, sq[:, :], True, True, [sqk, "c_onesb"], [p2k])
                        rsb, rsk_ = SCr.next()
                        rs = rsb[:, 0:T]
                        act(rs, p2[:, :], AF.Ln, [p2k], [rsk_], bias=EPS, scale=1.0 / 128)
                        act(rs, rs, AF.Exp, [rsk_], [rsk_], scale=-0.5)
                        if which == 0:
                            stt(PQ[:, hd % 4, :], ps[:, :], PV[:, l, C_QG:C_QG + 1], rs, ALU.mult, ALU.mult,
                                [pk, rsk_] + PVK(l), [("PQ", hd % 4)])
                        else:
                            stt(KT[:, hd, tok0:tok0 + T], ps[:, :], PV[:, l, C_KN:C_KN + 1], rs, ALU.mult, ALU.mult,
                                [pk, rsk_] + PVK(l), [("KT", hd, j)])
                vg = hp
                g = 16 + vg
                slot, skey = ws.get(WIN[l, g], ("scr", "win", l, g), 2048)
                for tb in range(4):
                    ps, pk = PG.next()
                    for kc in range(8):
                        mm(ps[:, 0:256], HT[:, kc, tb * 128:(tb + 1) * 128], wsl(slot, kc * 256, kc * 256 + 256),
                           kc == 0, kc == 7, [skey, ("HT", kc)], [pk])
                    eng = "act" if tb % 2 == 0 else "dve"
                    cp(eng, VC[:, 4 * j + tb, vg * 256:(vg + 1) * 256], ps[:, 0:256], [pk], [("VC", 4 * j + tb, vg)])
                for hh in range(2):
                    scores(l, j, 2 * hp + hh)

        def scores(l, j, hd):
            if True:
                po, pok = PO.next()
                rsum, rsk = RSUM[:, 0, :], ("rsum", 0)
                memset("pool", rsum[:, :], 0.0, [rsk])
                first = True
                for kb in range(4 * j + 3, -1, -1):
                    diag = kb >= 4 * j
                    off = (kb - 4 * j) * 128 if diag else 0
                    N = T - off
                    pz, pzk = PZ.next()
                    kblk = KT[:, hd, kb * 128:(kb + 1) * 128]
                    zr = [("KT", hd, kb // 4), ("PQ", hd % 4)]
                    if not diag:
                        mm(pz[:, 0:N], kblk, PQ[:, hd % 4, off:T], True, True, zr, [pzk])
                    else:
                        mm(pz[:, 0:128], kblk, PQ[:, hd % 4, off:off + 128], True, False, zr, [pzk])
                        mm(pz[:, 0:128], IDENTB, NEGB, False, True, ["c_idb", "c_negb"], [pzk])
                        if N > 128:
                            mm(pz[:, 128:N], kblk, PQ[:, hd % 4, off + 128:T], True, True, zr, [pzk])
                    s_, sk = SCr.next()
                    act(s_[:, 0:N], pz[:, 0:N], AF.Exp, [pzk], [sk], scale=-1.0)
                    act(s_[:, 0:N], s_[:, 0:N], AF.Ln, [sk], [sk], bias=1.0)
                    nl, nlk = SBr.next()
                    tt("dve", nl[:, 0:N], s_[:, 0:N], pz[:, 0:N], ALU.add, [sk, pzk], [nlk])
                    pa, pak = PA.next()
                    mm(pa[:, 0:N], UMAT, nl[:, 0:N], True, first, [nlk, "c_umat"], [pak])
                    if not first:
                        mm(pa[:, 0:N], ONESB, rsum[:, off:T], False, True, [rsk, "c_onesb"], [pak])
                    tt("dve", s_[:, 0:N], s_[:, 0:N], pa[:, 0:N], ALU.add, [sk, pak], [sk])
                    w_, wk = SBr.next()
                    act(w_[:, 0:N], s_[:, 0:N], AF.Exp, [sk], [wk], scale=-1.0)
                    mm(po[:, off:T], VC[:, kb, hd * 128:(hd + 1) * 128], w_[:, 0:N], first, kb == 0,
                       [("VC", kb, hd // 2), wk], [pok], skip=True)
                    if kb > 0:
                        tt("pool", rsum[:, off:T], rsum[:, off:T], nl[:, 0:N], ALU.add, [rsk, nlk], [rsk])
                    first = False
                cp("act", UY[:, 16 + hd, :], po[:, :], [pok], [("UY", 16 + hd)])
                merge01(l, j, hd)

        def gate_branch(l, oc, b, PR):
            slot, skey = ws.get(WMG[l, oc, b], ("scr", "wmg", l, oc, b), 2048)
            pg, pgk = PR.next()
            pb, pbk = PR.next()
            for kc in range(8):
                a = kc * 256
                mm(pg[:, :], wsl(slot, a, a + 128), HT[:, kc, :], kc == 0, kc == 7, [skey, ("HT", kc)], [pgk])
            for kc in range(8):
                a = kc * 256 + 128
                mm(pb[:, :], wsl(slot, a, a + 128), UY[:, 8 * b + kc, :], kc == 0, kc == 7,
                   [skey, ("UY", 8 * b + kc)], [pbk])
            gt, gk = SCr.next()
            sigmoid3(gt[:, 0:T], pg[:, :], PV[:, l, C_BG + 8 * b + oc:C_BG + 8 * b + oc + 1], [pgk], gk, PVK(l))
            tt("dve", gt[:, 0:T], gt[:, 0:T], pb[:, :], ALU.mult, [gk, pbk], [gk])
            return gt, gk

        def merge01(l, j, oc):
            g0, g0k = gate_branch(l, oc, 0, PX)
            g1, g1k = gate_branch(l, oc, 1, PX)
            tt("pool", MB[:, oc, :], g0[:, 0:T], g1[:, 0:T], ALU.add, [g0k, g1k], [("MB", oc)])

        def merge_and_out(l, j):
            tok = slice(j * T, (j + 1) * T)
            for oc in range(8):
                g2, g2k = gate_branch(l, oc, 2, PG)
                tt("pool", MB[:, oc, :], MB[:, oc, :], g2[:, 0:T], ALU.add, [("MB", oc), g2k], [("MB", oc)])
            for g in range(4):
                slot, skey = ws.get(WOU[l, g], ("scr", "wou", l, g), 2048)
                for cc in range(2):
                    oc = 2 * g + cc
                    ps, pk = proj(slot, skey, cc * 128, 256, lambda kc: MB[:, kc, :], lambda kc: ("MB", kc))
                    tt("dve", XT[:, oc, tok], XT[:, oc, tok], ps[:, :], ALU.add, [("XT", oc, j), pk], [("XT", oc, j)])

        def ffn_steps(l, j):
            tok = slice(j * T, (j + 1) * T)
            steps = []

            def gu(fc, fi):
                slot, skey = ws.get(WGU[l, fc], ("scr", "wgu", l, fc), 2048)
                pg, pgk = proj(slot, skey, 0, 256, H2c, H2k)
                pu, puk = proj(slot, skey, 128, 256, H2c, H2k)
                sg, sgk = SCr.next()
                act(sg[:, 0:T], pg[:, :], AF.Exp, [pgk], [sgk], scale=-1.0)
                act(sg[:, 0:T], sg[:, 0:T], AF.Ln, [sgk], [sgk], bias=1.0)
                act(sg[:, 0:T], sg[:, 0:T], AF.Exp, [sgk], [sgk], scale=-1.0)
                tt("dve", sg[:, 0:T], sg[:, 0:T], pg[:, :], ALU.mult, [sgk, pgk], [sgk])
                tt("dve", ATc(fi), sg[:, 0:T], pu[:, :], ALU.mult, [sgk, puk], [ATk(fi)])

            def dn(oc, p_):
                ps, pk = PG.next()
                slot, skey = ws.get(WDN[l, oc, p_], ("scr", "wdn", l, oc, p_), 1408)
                for kk in range(11):
                    mm(ps[:, :], wsl(slot, kk * 128, kk * 128 + 128), ATc(kk),
                       kk == 0, kk == 10, [skey, ATk(kk)], [pk])
                tt("dve", XT[:, oc, tok], XT[:, oc, tok], ps[:, :], ALU.add, [("XT", oc, j), pk], [("XT", oc, j)])

            for p_ in range(2):
                for fi in range(11):
                    steps.append(lambda fc=11 * p_ + fi, fi=fi: gu(fc, fi))
                for oc in range(8):
                    steps.append(lambda oc=oc, p_=p_: dn(oc, p_))
            return steps

        def ffn(l, j):
            for st_ in ffn_steps(l, j):
                st_()

        def early(l, j):
            P.phase = 'rmsnorm'
            rmsnorm(l, j, C_NM)
            P.phase = 'pool_branch'
            pool_branch(l, j)
            P.phase = 'lru_branch'
            lru_branch(l, j)

        def late1(l, j):
            P.phase = 'attn_branch'
            attn_branch(l, j)
            P.phase = 'merge_and_out'
            merge_and_out(l, j)
            P.phase = 'rmsnorm2'
            rmsnorm(l, j, C_NF, H2c, H2k)

        def late2(s, l, j):
            P.phase = 'ffn'
            ffn(l, j)
            if l == NL - 1:
                P.phase = 'store_y'
                store_y(s, j)

        for s in range(NSEQ):
            P.phase = 'load'
            load_x(s)
            stages = [(l, j) for l in range(NL) for j in range(NT)]
            if PIPE:
                early(*stages[0])
                for i, (l, j) in enumerate(stages):
                    late1(l, j)
                    fsteps = ffn_steps(l, j)
                    esteps = []
                    if i + 1 < len(stages):
                        l2, j2 = stages[i + 1]
                        P.phase = 'rmsnorm'
                        rmsnorm(l2, j2, C_NM)
                        for g in range(4):
                            esteps.append(('pool_branch', lambda g=g: pool_unit(l2, j2, g)))
                        for g in range(4):
                            esteps.append(('lru_branch', lambda g=g: lru_unit(l2, j2, g)))
                    for k, fs in enumerate(fsteps):
                        if k % ESTRIDE == 0 and esteps:
                            ph, es_ = esteps.pop(0)
                            P.phase = ph
                            es_()
                        P.phase = 'ffn'
                        fs()
                    for ph, es_ in esteps:
                        P.phase = ph
                        es_()
                    if l == NL - 1:
                        P.phase = 'store_y'
                        store_y(s, j)
            else:
                for (l, j) in stages:
                    early(l, j)
                    late1(l, j)
                    late2(s, l, j)
        P.add("sp", None, [("yout", s, j, tb) for s in range(NSEQ) for j in range(NT) for tb in range(4)], ())

    saved = (P.ops, P.pages, list(wcount), list(stcnt), stg_i[0], wsmcnt[0])
    P.ops, P.pages = [], {}
    rots = [PG, PZ, PA, PO, PX]
    rsave = [r.i for r in rots]
    ws1 = WStream(None)
    record_all(ws1)
    log = ws1.req
    P.ops, P.pages = saved[0], saved[1]
    wcount[:] = saved[2]
    stcnt[:] = saved[3]
    stg_i[0] = saved[4]
    wsmcnt[0] = saved[5]
    for r, i in zip(rots, rsave):
        r.i = i
    record_all(WStream(log))

    if speed:
        for o_ in P.ops:
            if o_.eng in speed and not o_.is_dma:
                o_.occ *= speed[o_.eng]
                o_.lat *= speed[o_.eng]
    P.sync_lat = sync_lat
    if FILL:
        fill_rhs = CONB[:, :, :].rearrange("p a b -> p (a b)")
        fill_ps = psum[FILL_BANK]
        P.filler = (FILL[0], FILL[1], FILL[2],
                    lambda e: e.matmul(fill_ps[:, :], ONESB, fill_rhs, start=True, stop=True))
    P.use_blevel = blevel
    P.finalize(new_sem)
    if not emit:
        es.close()
        return nc, P
    with nc.Block() as block:
        @block.tensor
        def _(e):
            P.emit("pe", e)

        @block.scalar
        def _(e):
            P.emit("act", e)

        @block.vector
        def _(e):
            P.emit("dve", e)

        @block.gpsimd
        def _(e):
            P.emit("pool", e)

        @block.sync
        def _(e):
            P.emit("sp", e)
    es.close()
    return nc, P


_CACHE = {}


def kernel(**inputs):
    ncores = 8
    if "nc" not in _CACHE:
        _CACHE["nc"] = build()[0]
    nc = _CACHE["nc"]
    x = np.ascontiguousarray(np.asarray(inputs["x"], dtype=np.float32))
    B = x.shape[0]
    per = B // ncores
    in_maps = []
    for c in range(ncores):
        m = {}
        for k, v in inputs.items():
            if k == "x":
                m["x"] = np.ascontiguousarray(x[c * per:(c + 1) * per])
            else:
                m[k] = np.ascontiguousarray(np.asarray(v, dtype=np.float32))
        in_maps.append(m)
    res = run_bass_kernel_spmd(nc, in_maps, core_ids=list(range(ncores)))
    out = np.concatenate([np.asarray(r["y"]) for r in res.results], axis=0)
    return out.astype(np.float32)
```

```python
import numpy as np
from contextlib import ExitStack
import concourse.bass as bass
import concourse.mybir as mybir
from concourse.bass_utils import run_bass_kernel_spmd

F32 = mybir.dt.float32
BF16 = mybir.dt.bfloat16
AF = mybir.ActivationFunctionType
ALU = mybir.AluOpType

D = 1024
SEQ = 2048
T = 512
DFF = 2816
NFC = 22
EPS = 1e-6
WINDOWS = (2, 4, 8, 16)
EPOCH = 6000


class Op:
    __slots__ = ("eng", "fn", "is_dma", "signal", "deps", "need_sig", "waits", "inc", "occ", "lat", "idx", "tag", "st", "crit", "rdep")

    def __init__(self, eng, fn, is_dma, signal, occ=300.0, lat=None):
        self.occ = occ
        self.lat = occ if lat is None else lat
        self.eng = eng
        self.fn = fn
        self.is_dma = is_dma
        self.signal = signal
        self.deps = {}
        self.need_sig = False
        self.waits = []
        self.inc = 16 if is_dma else 1


class Prog:
    def __init__(self):
        self.ops = []
        self.pages = {}

    def add(self, eng, fn, reads=(), writes=(), dma_signal=None, occ=300.0, lat=None):
        op = Op(eng, fn, dma_signal is not None, dma_signal, occ, lat)
        op.tag = getattr(self, "phase", "")
        op.rdep = None
        deps = op.deps
        is_dma = dma_signal is not None
        for k in reads:
            st = self.pages.get(k)
            if st is None:
                st = self.pages[k] = [{}, {}]
            for w in st[0].values():
                deps[id(w)] = (w, True)
            if fn is not None:
                rkey = id(op) if is_dma else eng
                prev = st[1].get(rkey)
                if prev is not None and prev is not op and id(prev) not in deps:
                    deps[id(prev)] = (prev, None)
                st[1][rkey] = op
        for k in writes:
            st = self.pages.get(k)
            if st is None:
                st = self.pages[k] = [{}, {}]
            for w in st[0].values():
                if id(w) not in deps:
                    deps[id(w)] = (w, False)
            for r in st[1].values():
                if r is not op and (id(r) not in deps or deps[id(r)][1] is None):
                    deps[id(r)] = (r, False)
            st[1] = {}
            wkey = ("dma", id(dma_signal[0])) if is_dma else eng
            st[0][wkey] = op
        self.ops.append(op)
        return op

    def schedule(self, sync_lat=250.0):
        import heapq
        ops = self.ops
        n = len(ops)
        for i, op in enumerate(ops):
            op.idx = i
        groups = {}
        for op in ops:
            if op.is_dma:
                groups.setdefault((id(op.signal[0]), op.signal[1]), []).append(op)
        for op in ops:
            extra = []
            okey = (id(op.signal[0]), op.signal[1]) if op.is_dma else None
            for d, raw in op.deps.values():
                if d.is_dma and raw is not None:
                    dkey = (id(d.signal[0]), d.signal[1])
                    if dkey == okey:
                        continue
                    grp = groups[dkey]
                    if len(grp) > 1:
                        for m_ in grp:
                            if m_ is not d and m_ is not op and id(m_) not in op.deps and m_.idx < op.idx:
                                extra.append((m_, raw))
            for m_, raw in extra:
                op.deps[id(m_)] = (m_, raw)
        succ = [[] for _ in range(n)]
        indeg = [0] * n
        for op in ops:
            for d, raw in op.deps.values():
                succ[d.idx].append(op.idx)
                indeg[op.idx] += 1
        prio = [0.0] * n
        if getattr(self, "use_blevel", False):
            bl = [0.0] * n
            for i in range(n - 1, -1, -1):
                m_ = 0.0
                for j in succ[i]:
                    if bl[j] > m_:
                        m_ = bl[j]
                bl[i] = m_ + ops[i].lat + 100.0
            for i in range(n):
                prio[i] = -bl[i]
        else:
            for i in range(n):
                prio[i] = float(i)
        engs = sorted(set(op.eng for op in ops))
        free = {e: 0.0 for e in engs}
        ready = {e: [] for e in engs}
        avail = {e: [] for e in engs}
        finish = [0.0] * n
        efree = [0.0] * n
        rtime = [0.0] * n
        order = {e: [] for e in engs}
        for op in ops:
            if indeg[op.idx] == 0:
                heapq.heappush(ready[op.eng], (0.0, op.idx))
        done = 0
        while done < n:
            best = None
            for e in engs:
                rq, aq = ready[e], avail[e]
                while rq and rq[0][0] <= free[e]:
                    i_ = heapq.heappop(rq)[1]
                    heapq.heappush(aq, (prio[i_], i_))
                if aq:
                    cand = (free[e], aq[0][1], e, True)
                elif rq:
                    cand = (rq[0][0], rq[0][1], e, False)
                else:
                    continue
                if best is None or cand < best:
                    best = cand
            st, i, e, from_avail = best
            if from_avail:
                heapq.heappop(avail[e])
            else:
                heapq.heappop(ready[e])
            op = ops[i]
            op.st = st
            prev_on_eng = order[e][-1] if order[e] else None
            if prev_on_eng is not None and free[e] >= rtime[i]:
                op.crit = ("res", prev_on_eng)
            else:
                op.crit = ("dep", getattr(op, "rdep", None))
            free[e] = st + op.occ
            efree[i] = st + op.occ
            finish[i] = st + op.lat
            order[e].append(op)
            done += 1
            for j in succ[i]:
                o2 = ops[j]
                same = (o2.eng == op.eng and not op.is_dma and not o2.is_dma)
                if same and (op.eng == "pe" or o2.deps[id(op)][1] is None):
                    t = efree[i]
                elif same:
                    t = finish[i] + 80.0
                else:
                    t = finish[i] + sync_lat
                if t > rtime[j]:
                    rtime[j] = t
                    o2.rdep = op
                indeg[j] -= 1
                if indeg[j] == 0:
                    heapq.heappush(ready[o2.eng], (rtime[j], j))
        nfill = 0
        ff = getattr(self, "filler", None)
        if ff is not None:
            frac, cap, mingap, fn_ = ff
            newpe = []
            prev_end = 0.0
            for op in order["pe"]:
                gap = op.st - prev_end
                if gap > mingap and op.st > 60000.0:
                    k = min(int(frac * gap / 220.0), cap)
                    for _ in range(k):
                        fo = Op("pe", fn_, False, None, 216.0, 216.0)
                        fo.tag = "filler"
                        fo.st = prev_end
                        newpe.append(fo)
                        nfill += 1
                newpe.append(op)
                prev_end = op.st + op.occ
            order["pe"] = newpe
        self.nfill = nfill
        self.order = order
        self.makespan = max(finish) if n else 0.0
        return order

    def finalize(self, new_sem, reorder=True):
        if reorder:
            self.schedule(getattr(self, 'sync_lat', 250.0))
            newops = []
            for e in self.order:
                newops.extend(self.order[e])
            self.ops = newops
        for op in self.ops:
            for d, raw in op.deps.values():
                if raw is None:
                    continue
                if d.is_dma and op.is_dma and d.signal[0] is op.signal[0] and d.signal[1] == op.signal[1]:
                    continue
                if d.eng == op.eng and not d.is_dma and not op.is_dma:
                    if op.eng == "pe":
                        continue
                d.need_sig = True
                op.waits.append(d)
        cnt = {}
        sems = {}
        for op in self.ops:
            if op.is_dma or not op.need_sig:
                continue
            n = cnt.get(op.eng, 0)
            cnt[op.eng] = n + 1
            key = (op.eng, n // EPOCH)
            if key not in sems:
                sems[key] = new_sem("s_%s_%d" % key)
            op.signal = (sems[key], n % EPOCH + 1)

    def emit(self, eng_name, e):
        waited = {}
        for op in self.ops:
            if op.eng != eng_name:
                continue
            for d in op.waits:
                sem, v = d.signal
                if waited.get(id(sem), 0) < v:
                    e.wait_ge(sem, v)
                    waited[id(sem)] = v
            if op.fn is None:
                continue
            ins = op.fn(e)
            if op.is_dma or op.need_sig:
                ins.then_inc(op.signal[0], op.inc)


class Rot:
    def __init__(self, items):
        self.items = items
        self.i = 0

    def next(self):
        it = self.items[self.i % len(self.items)]
        self.i += 1
        return it


def build(NSEQ=2, NL=2, NT=4, NSLOT=3, NSC=7, NSB=4, whatif=False, blevel=True, PIPE=True, DEEP=True, ESTRIDE=4, EORDER=0, FILL=(0.6, 12, 500.0), FILL_BANK=5, sync_lat=320.0, PSCFG=([0, 1, 2, 3, 6, 7], [0, 1], [2, 3], [4]), speed=None, emit=True, PE_A=8.0, PE_B=0.417):
    nc = bass.Bass("TRN2", target_bir_lowering=False)
    S_ = NT * T
    dram = {}

    def din(name, shape):
        dram[name] = nc.dram_tensor(name, list(shape), F32, kind="ExternalInput").ap()
        return dram[name]

    x = din("x", [NSEQ, S_, D])
    norm_mix = din("norm_mix", [NL, D])
    w_in = din("w_in", [NL, D, 8192])
    b_gate = din("b_gate", [NL, 3 * D])
    w_pool = din("w_pool", [NL, 4, 256, 256])
    pool_scale = din("pool_scale", [NL, D])
    conv_w = din("conv_w", [NL, 4, D])
    conv_b = din("conv_b", [NL, D])
    w_rg = din("w_rg", [NL, 16, 64, 64])
    b_rg = din("b_rg", [NL, D])
    w_ig = din("w_ig", [NL, 16, 64, 64])
    b_ig = din("b_ig", [NL, D])
    lru_lambda = din("lru_lambda", [NL, D])
    q_norm = din("q_norm", [NL, 128])
    k_norm = din("k_norm", [NL, 128])
    w_branch = din("w_branch", [NL, 3, D, D])
    w_out = din("w_out", [NL, D, D])
    norm_ffn = din("norm_ffn", [NL, D])
    w_ffn_gate = din("w_ffn_gate", [NL, D, DFF])
    w_ffn_up = din("w_ffn_up", [NL, D, DFF])
    w_ffn_down = din("w_ffn_down", [NL, DFF, D])
    y = nc.dram_tensor("y", [NSEQ, S_, D], F32, kind="ExternalOutput").ap()
    dbg_out = {}

    def scr(name, shape):
        return nc.dram_tensor(name, list(shape), BF16, kind="Internal").ap()

    WIN = scr("s_win", [NL, 20, 128, 2048])
    WPO = scr("s_wpo", [NL, 4, 128, 512])
    WBD = scr("s_wbd", [NL, 4, 128, 512])
    WMG = scr("s_wmg", [NL, 8, 3, 128, 2048])
    WOU = scr("s_wou", [NL, 4, 128, 2048])
    WGU = scr("s_wgu", [NL, NFC, 128, 2048])
    WDN = scr("s_wdn", [NL, 8, 2, 128, 1408])

    P = Prog()
    es = ExitStack()

    def sb(name, shape, dt):
        return es.enter_context(nc.sbuf_tensor(name, list(shape), dt))

    sem_list = []

    def new_sem(name):
        s = es.enter_context(nc.semaphore(name))
        sem_list.append(s)
        return s

    XT = sb("XT", [128, 8, SEQ], BF16 if whatif else F32)
    KT = sb("KT", [128, 8, SEQ], BF16)
    VC = sb("VC", [128, 16, 1024], BF16)
    HT = sb("HT", [128, 8, T], BF16)
    UY = sb("UY", [128, 24, T], BF16)
    PQ = sb("PQ", [128, 4, T], BF16)
    MB = sb("MB", [128, 8, T], BF16)
    WS = sb("WS", [128, NSLOT, 2048], BF16)
    PV = sb("PV", [128, NL, 132], F32)
    IDT = sb("IDT", [128, 128], F32)
    CONB = sb("CONB", [128, 4, 128], BF16)
    RSUM = sb("RSUM", [128, 1, T], BF16)
    WSM = sb("WSM", [128, 512], BF16)
    ZB = RSUM[:, 0, :]
    RC = sb("RC", [128, 4, 16], F32)
    UPH = sb("UPH", [128, 8, 15], F32)
    ULH = sb("ULH", [128, 8, 4], F32)
    HST = sb("HST", [128, 8], F32)
    SCF = sb("SC", [128, NSC * 528], F32)

    class _SC:
        def __getitem__(self, idx):
            p_, i_, c_ = idx
            assert isinstance(i_, int)
            return SCF[p_, i_ * 528:(i_ + 1) * 528][:, c_]
    SC = _SC()

    class _STG:
        def __getitem__(self, idx):
            p_, b_, c_ = idx
            return SCF[p_, b_ * 1056:b_ * 1056 + 1024][:, c_]
    STG = _STG()
    STGK = lambda b: [("sc", 2 * b), ("sc", 2 * b + 1)]
    SCB = sb("SCB", [128, NSB, T], BF16)

    psum = [es.enter_context(nc.psum_tensor("ps%d" % i, [128, 512], F32)) for i in range(8)]

    def psrot(ids):
        return Rot([(psum[i], ("ps", i)) for i in ids])

    PG = psrot(PSCFG[0])
    PZ = psrot(PSCFG[1])
    PA = psrot(PSCFG[2])
    PO = psrot(PSCFG[3])
    PX = psrot([6, 7])

    IOT = SC[:, 3, 0:16]
    ONESF = SC[:, 0, 0:128]
    IDENT = IDT[:, :]
    MASKF = SC[:, 1, 0:128]
    UF = SC[:, 2, 0:128]
    ONESB = CONB[:, 0, :]
    NEGB = CONB[:, 1, :]
    UMAT = CONB[:, 2, :]
    IDENTB = CONB[:, 3, :]

    def fsz(ap_):
        n_ = 1
        for d_ in ap_.shape[1:]:
            n_ *= int(d_)
        return n_

    def mm(out, lhsT, rhs, start, stop, reads, writes, skip=False):
        n_ = fsz(rhs)
        P.add("pe", lambda e: e.matmul(out, lhsT, rhs, start=start, stop=stop, skip_group_check=skip),
              reads, writes, occ=PE_A + max(n_, 64) * PE_B, lat=200.0 + n_ * PE_B)

    def act(out, in_, func, reads, writes, bias=None, scale=None):
        kw = {}
        if bias is not None:
            kw["bias"] = bias
        if scale is not None:
            kw["scale"] = scale
        P.add("act", lambda e: e.activation(out, in_, func, **kw), reads, writes, occ=150.0 + fsz(out) * 0.75)

    def sigmoid3(out, in_, negb, rin, wkey, extra):
        act(out, in_, AF.Exp, rin + extra, [wkey], bias=negb, scale=-1.0)
        act(out, out, AF.Ln, [wkey], [wkey], bias=1.0)
        act(out, out, AF.Exp, [wkey], [wkey], scale=-1.0)

    def tt(eng, out, in0, in1, op, reads, writes):
        P.add(eng, lambda e: e.tensor_tensor(out, in0, in1, op), reads, writes,
              occ=(100.0 + fsz(out) * 0.95) if eng == "dve" else (200.0 + fsz(out) * 1.66))

    def stt(out, in0, scalar, in1, op0, op1, reads, writes):
        P.add("dve", lambda e: e.scalar_tensor_tensor(out, in0, scalar, in1, op0, op1), reads, writes,
              occ=160.0 + fsz(out) * 1.05)

    def ts(eng, out, in0, s1, s2, op0, op1, reads, writes):
        oc_ = (160.0 + fsz(out) * 0.6) if eng == "dve" else (250.0 + fsz(out) * 1.7)
        if op1 is None:
            P.add(eng, lambda e: e.tensor_scalar(out, in0, s1, None, op0), reads, writes, occ=oc_)
        else:
            P.add(eng, lambda e: e.tensor_scalar(out, in0, s1, s2, op0, op1), reads, writes, occ=oc_)

    def cp(eng, out, in_, reads, writes):
        if eng == "act":
            P.add("act", lambda e: e.copy(out, in_), reads, writes, occ=260.0 + fsz(out) * 0.85)
        else:
            P.add(eng, lambda e: e.tensor_copy(out, in_), reads, writes,
                  occ=(160.0 + fsz(out) * 0.8) if eng == "dve" else (250.0 + fsz(out) * 1.5))

    def memset(eng, ap, val, writes):
        P.add(eng, lambda e: e.memset(ap, val), (), writes, occ=160.0 + fsz(ap) * 1.0)

    def dma(q, out, in_, reads, writes, signal, **kw):
        nel = 1
        for d_ in out.shape:
            nel *= int(d_)
        P.add(q, lambda e: e.dma_start(out=out, in_=in_, **kw), reads, writes, dma_signal=signal,
              occ=(1500.0 if q == "pool" else 120.0), lat=2200.0 + nel * 2 * 0.006)

    CK = [("sc", 0)]
    memset("pool", ONESF, 1.0, CK)
    P.add("pool", lambda e: e.affine_select(IDENT, ONESF, [[1, 128]], ALU.is_equal, 0.0, base=0,
                                            channel_multiplier=-1), CK, ["c_id"])
    P.add("pool", lambda e: e.affine_select(MASKF, ONESF, [[1, 128]], ALU.is_gt, 0.0, base=0,
                                            channel_multiplier=-1), CK, [("sc", 1)])
    P.add("pool", lambda e: e.affine_select(UF, ONESF, [[-1, 128]], ALU.is_gt, 0.0, base=0,
                                            channel_multiplier=1), CK, [("sc", 2)])
    P.add("pool", lambda e: e.iota(IOT, [[1, 16]], base=1, channel_multiplier=0,
                                   allow_small_or_imprecise_dtypes=True), (), [("sc", 3)])
    memset("pool", ZB, 0.0, [("rsum", 0)])
    cp("dve", ONESB, ONESF, CK, ["c_onesb"])
    ts("dve", NEGB, MASKF, 28.0, -28.0, ALU.mult, ALU.add, [("sc", 1)], ["c_negb"])
    cp("dve", IDENTB, IDENT, ["c_id"], ["c_idb"])
    cp("dve", UMAT, UF, [("sc", 2)], ["c_umat"])
    for g, w in enumerate(WINDOWS):
        ts("dve", RC[:, g, :], IOT, float(w), None, ALU.min, None, [("sc", 3)], [("c_rc", g)])
        P.add("dve", lambda e, g=g: e.reciprocal(RC[:, g, :], RC[:, g, :]), [("c_rc", g)], [("c_rc", g)])

    C_NM, C_PS, C_CW, C_CB, C_BRG, C_BIG, C_LAM, C_NF, C_BG, C_QN, C_KN, C_NSP, C_NSP2, C_QG, C_TMP = (
        0, 8, 16, 48, 56, 64, 72, 80, 88, 112, 113, 114, 122, 130, 72)
    psem = new_sem("psem")
    pl = []
    for l in range(NL):
        def vec8(src):
            return src.rearrange("(c p) -> p c", p=128)
        pl.append((PV[:, l, C_NM:C_NM + 8], vec8(norm_mix[l])))
        pl.append((PV[:, l, C_PS:C_PS + 8], vec8(pool_scale[l])))
        for k in range(4):
            pl.append((PV[:, l, C_CW + 8 * k:C_CW + 8 * k + 8], vec8(conv_w[l, k])))
        pl.append((PV[:, l, C_CB:C_CB + 8], vec8(conv_b[l])))
        pl.append((PV[:, l, C_BRG:C_BRG + 8], vec8(b_rg[l])))
        pl.append((PV[:, l, C_BIG:C_BIG + 8], vec8(b_ig[l])))
        pl.append((PV[:, l, C_LAM:C_LAM + 8], vec8(lru_lambda[l])))
        pl.append((PV[:, l, C_NF:C_NF + 8], vec8(norm_ffn[l])))
        pl.append((PV[:, l, C_BG:C_BG + 24], b_gate[l].rearrange("(c p) -> p c", p=128)))
        pl.append((PV[:, l, C_QN:C_QN + 1], q_norm[l].rearrange("(p o) -> p o", o=1)))
        pl.append((PV[:, l, C_KN:C_KN + 1], k_norm[l].rearrange("(p o) -> p o", o=1)))
    ptotal = 16 * len(pl)
    for o_, i_ in pl:
        dma("sp", o_, i_, (), ["pv_raw"], (psem, ptotal), allow_slow_non_contiguous=True)
    for l in range(NL):
        act(PV[:, l, C_TMP:C_TMP + 8], PV[:, l, C_LAM:C_LAM + 8], AF.Exp, ["pv_raw"], [("pv_t", l)], scale=-1.0)
        act(PV[:, l, C_TMP:C_TMP + 8], PV[:, l, C_TMP:C_TMP + 8], AF.Ln, [("pv_t", l)], [("pv_t", l)], bias=1.0)
        ts("dve", PV[:, l, C_NSP:C_NSP + 8], PV[:, l, C_TMP:C_TMP + 8], -8.0, None, ALU.mult, None,
           [("pv_t", l)], [("pv_d", l)])
        ts("dve", PV[:, l, C_NSP2:C_NSP2 + 8], PV[:, l, C_TMP:C_TMP + 8], -16.0, None, ALU.mult, None,
           [("pv_t", l)], [("pv_d", l)])
        ts("dve", PV[:, l, C_QG:C_QG + 1], PV[:, l, C_QN:C_QN + 1], 128.0 ** -0.5, None, ALU.mult, None,
           ["pv_raw"], [("pv_d", l)])
        ts("dve", PV[:, l, C_BRG:C_BRG + 16], PV[:, l, C_BRG:C_BRG + 16], -1.0, None, ALU.mult, None,
           ["pv_raw"], ["pv_raw"])
        ts("dve", PV[:, l, C_BG:C_BG + 24], PV[:, l, C_BG:C_BG + 24], -1.0, None, ALU.mult, None,
           ["pv_raw"], ["pv_raw"])
    PVK = lambda l: ["pv_raw", ("pv_d", l)]

    prev_batch = [None]

    def conv_batch(name, items, maxn=8):
        for i0 in range(0, len(items), maxn):
            sub = items[i0:i0 + maxn]
            sem = new_sem("cv_%s_%d" % (name, i0))
            tot = 16 * len(sub)
            rd = [prev_batch[0]] if prev_batch[0] is not None else []
            for dst, src, key in sub:
                dma("pool", dst, src, rd, [key], (sem, tot))
            prev_batch[0] = sub[-1][2]

    def kview(ap_, kc):
        return ap_.rearrange("(kc p) c -> p kc c", p=128)

    def convert_layer(l):
        for b5 in range(5):
            items = []
            for g in range(4 * b5, 4 * b5 + 4):
                items.append((WIN[l, g].rearrange("p (kc c) -> p kc c", kc=8),
                              kview(w_in[l][:, g * 256:(g + 1) * 256], 8), ("scr", "win", l, g)))
            conv_batch("win%d_%d" % (l, b5), items)
            if b5 == 0:
                items = []
                for g in range(4):
                    items.append((WPO[l, g].rearrange("p (ic j) -> p ic j", ic=2),
                                  w_pool[l, g].rearrange("(ic p) j -> p ic j", p=128), ("scr", "wpo", l, g)))
                conv_batch("wpo%d" % l, items)
                zsem = new_sem("zf%d" % l)
                for gi in range(4):
                    dma("pool", WBD[l, gi], ZB, [("rsum", 0), prev_batch[0]], [("scr", "wbd", l, gi)], (zsem, 64))
                prev_batch[0] = ("scr", "wbd", l, 3)
                items = []
                for gi in range(4):
                    dstv = WBD[l, gi].rearrange("p (t cc c) -> p t cc c", t=2, cc=2)
                    for t_, wsrc in enumerate((w_rg, w_ig)):
                        for e_ in range(2):
                            h0 = 4 * gi + e_
                            w4 = wsrc[l].rearrange("(c e) i j -> e c i j", e=2)
                            src = w4[e_, 2 * gi:2 * gi + 2].rearrange("c i j -> i c j")
                            dst = dstv[e_ * 64:(e_ + 1) * 64, t_, :, e_ * 64:(e_ + 1) * 64]
                            items.append((dst, src, ("scr", "wbd", l, gi)))
                conv_batch("wbd%d" % l, items)
        for q4 in range(4):
            items = []
            for oc in range(2 * q4, 2 * q4 + 2):
                for b in range(3):
                    dv = WMG[l, oc, b].rearrange("p (kc t c) -> p kc t c", kc=8, t=2)
                    c0 = 5120 + b * 1024 + oc * 128
                    items.append((dv[:, :, 0, :], kview(w_in[l][:, c0:c0 + 128], 8), ("scr", "wmg", l, oc, b)))
                    items.append((dv[:, :, 1, :], kview(w_branch[l, b][:, oc * 128:(oc + 1) * 128], 8),
                                  ("scr", "wmg", l, oc, b)))
            conv_batch("wmg%d_%d" % (l, q4), items)
        items = []
        for g in range(4):
            items.append((WOU[l, g].rearrange("p (kc c) -> p kc c", kc=8),
                          kview(w_out[l][:, g * 256:(g + 1) * 256], 8), ("scr", "wou", l, g)))
        conv_batch("wou%d" % l, items)
        for hb in range(2):
            items = []
            for fc in range(11 * hb, 11 * hb + 11):
                dv = WGU[l, fc].rearrange("p (kc t c) -> p kc t c", kc=8, t=2)
                items.append((dv[:, :, 0, :], kview(w_ffn_gate[l][:, fc * 128:(fc + 1) * 128], 8),
                              ("scr", "wgu", l, fc)))
                items.append((dv[:, :, 1, :], kview(w_ffn_up[l][:, fc * 128:(fc + 1) * 128], 8),
                              ("scr", "wgu", l, fc)))
            conv_batch("wgu%d_%d" % (l, hb), items)
            items = []
            for oc in range(8):
                src = w_ffn_down[l][hb * 1408:(hb + 1) * 1408, oc * 128:(oc + 1) * 128]
                items.append((WDN[l, oc, hb].rearrange("p (kk c) -> p kk c", kk=11),
                              src.rearrange("(kk p) c -> p kk c", p=128), ("scr", "wdn", l, oc, hb)))
            conv_batch("wdn%d_%d" % (l, hb), items)

    for l in range(NL):
        convert_layer(l)

    wsems = [new_sem("ws%d" % i) for i in range(NSLOT)]
    wcount = [0] * NSLOT

    class WStream:
        def __init__(self, log):
            self.log = log
            self.req = []
            self.n = 0
            self.issued = 0

        def _issue(self, m):
            src, key, ne = self.log[m]
            s = m % NSLOT
            wcount[s] += 16
            dma("sp", WS[:, s, 0:ne], src, [key], [("ws", s)], (wsems[s], wcount[s]))

        def get(self, src, key, ne, prev_live=False):
            n = self.n
            self.n += 1
            if self.log is None:
                self.req.append((src, key, ne))
            else:
                depth = NSLOT - 1 if (prev_live or not DEEP) else NSLOT
                while self.issued < min(n + depth, len(self.log)):
                    self._issue(self.issued)
                    self.issued += 1
            s = n % NSLOT
            return s, ("ws", s)

    wsm_sem = new_sem("wsm")
    wsmcnt = [0]

    def load_wsm(src, key):
        wsmcnt[0] += 16
        dma("sp", WSM[:, 0:512], src, [key], ["wsm"], (wsm_sem, wsmcnt[0]))

    stsem = [new_sem("stg0"), new_sem("stg1")]
    stcnt = [0, 0]
    stg_i = [0]

    def record_all(ws):
        SCr = Rot([(SC[:, i, :], ("sc", i)) for i in range(NSC)])
        SBr = Rot([(SCB[:, i, :], ("scb", i)) for i in range(NSB)])

        def wsl(s, a, b):
            return WS[:, s, a:b]

        def H2c(kc):
            return MB[:, 3 + kc, :] if kc < 5 else PQ[:, kc - 5, :]

        def H2k(kc):
            return ("MB", 3 + kc) if kc < 5 else ("PQ", kc - 5)

        def ATc(i):
            return UY[:, 16 + i, :] if i < 8 else MB[:, i - 8, :]

        def ATk(i):
            return ("UY", 16 + i) if i < 8 else ("MB", i - 8)

        def rmsnorm(l, j, gbase, dc=None, dk=None):
            dc = dc or (lambda c: HT[:, c, :])
            dk = dk or (lambda c: ("HT", c))
            tok = slice(j * T, (j + 1) * T)
            rsb, rsk_ = SCr.next()
            rs = rsb[:, 0:T]
            for c in range(8):
                act(dc(c), XT[:, c, tok], AF.Square, [("XT", c, j)], [dk(c)])
            ps, pk = PG.next()
            for c in range(8):
                mm(ps[:, :], ONESB, dc(c), c == 0, c == 7, [dk(c), "c_onesb"], [pk])
            act(rs, ps[:, :], AF.Ln, [pk], [rsk_], bias=EPS, scale=1.0 / D)
            act(rs, rs, AF.Exp, [rsk_], [rsk_], scale=-0.5)
            for c in range(8):
                stt(dc(c), XT[:, c, tok], PV[:, l, gbase + c:gbase + c + 1], rs, ALU.mult, ALU.mult,
                    [("XT", c, j), rsk_] + PVK(l), [dk(c)])

        def load_x(s):
            for tb in range(S_ // 128):
                b = stg_i[0] % 2
                stg_i[0] += 1
                stcnt[b] += 16
                dma("sp", STG[:, b, 0:1024], x[s, tb * 128:(tb + 1) * 128, :], (), STGK(b), (stsem[b], stcnt[b]))
                p0, k0 = PG.next()
                p1, k1 = PG.next()
                for cc in range(8):
                    pp, kk = (p0, k0) if cc < 4 else (p1, k1)
                    P.add("pe", lambda e, pp=pp, cc=cc, b=b: e.transpose(
                        pp[:, (cc % 4) * 128:(cc % 4 + 1) * 128], STG[:, b, cc * 128:(cc + 1) * 128], IDENT),
                        STGK(b) + ["c_id"], [kk])
                jt = tb // 4
                cp("act", XT[:, 0:4, tb * 128:(tb + 1) * 128], p0[:, :].rearrange("p (a b) -> p a b", a=4),
                   [k0], [("XT", c, jt) for c in range(4)])
                cp("dve", XT[:, 4:8, tb * 128:(tb + 1) * 128], p1[:, :].rearrange("p (a b) -> p a b", a=4),
                   [k1], [("XT", c, jt) for c in range(4, 8)])

        def store_y(s, j):
            for tb in range(4):
                t0 = j * T + tb * 128
                b = stg_i[0] % 2
                stg_i[0] += 1
                p0, k0 = PG.next()
                p1, k1 = PG.next()
                for cc in range(8):
                    pp, kk = (p0, k0) if cc < 4 else (p1, k1)
                    P.add("pe", lambda e, pp=pp, cc=cc, t0=t0: e.transpose(
                        pp[:, (cc % 4) * 128:(cc % 4 + 1) * 128], XT[:, cc, t0:t0 + 128], IDENT),
                        [("XT", cc, j), "c_id"], [kk])
                cp("act", STG[:, b, 0:512], p0[:, :], [k0], STGK(b))
                cp("dve", STG[:, b, 512:1024], p1[:, :], [k1], STGK(b))
                stcnt[b] += 16
                dma("sp", y[s, t0:t0 + 128, :], STG[:, b, 0:1024], STGK(b), [("yout", s, j, tb)],
                    (stsem[b], stcnt[b]))

        def proj(slot, skey, col0, ncols_slot, rhs_chunks, rkeys):
            ps, pk = PG.next()
            for kc in range(8):
                a = kc * ncols_slot + col0
                mm(ps[:, :], wsl(slot, a, a + 128), rhs_chunks(kc), kc == 0, kc == 7, [skey, rkeys(kc)], [pk])
            return ps, pk

        HTc = lambda kc: HT[:, kc, :]
        HTk = lambda kc: ("HT", kc)

        def pool_branch(l, j):
            for g in range(4):
                pool_unit(l, j, g)

        def pool_unit(l, j, g):
            if True:
                w = WINDOWS[g]
                slot, skey = ws.get(WIN[l, g], ("scr", "win", l, g), 2048)
                load_wsm(WPO[l, g], ("scr", "wpo", l, g))
                pts = [SBr.next(), SBr.next()]
                for cc in range(2):
                    c = 2 * g + cc
                    ps, pk = proj(slot, skey, cc * 128, 256, HTc, HTk)
                    ub, uk = SCr.next()
                    cp("act", ub[:, 16:528], ps[:, :], [pk], [uk])
                    if j == 0:
                        memset("dve", ub[:, 1:16], 0.0, [uk])
                    else:
                        cp("dve", ub[:, 1:16], UPH[:, c, :], [("uph", c)], [uk])
                    cp("dve", UPH[:, c, :], ub[:, 513:528], [uk], [("uph", c)])
                    src, sk = ub, uk
                    for lev in range(g + 1):
                        sh = 1 << lev
                        dst, dk = SCr.next()
                        lo = 2 * sh
                        tt("pool", dst[:, lo:528], src[:, lo:528], src[:, lo - sh:528 - sh], ALU.add, [sk], [dk])
                        src, sk = dst, dk
                    stt(pts[cc][0][:, :], src[:, 16:528], 1.0 / w, ub[:, 16:528], ALU.mult, ALU.subtract,
                        [sk, uk], [pts[cc][1]])
                    if j == 0:
                        t16, t16k = SCr.next()
                        tt("dve", t16[:, 0:16], src[:, 16:32], RC[:, g, :], ALU.mult, [sk, ("c_rc", g)], [t16k])
                        tt("dve", pts[cc][0][:, 0:16], t16[:, 0:16], ub[:, 16:32], ALU.subtract, [t16k, uk], [pts[cc][1]])
                for jc in range(2):
                    c = 2 * g + jc
                    ps, pk = PG.next()
                    for ic in range(2):
                        a = ic * 256 + jc * 128
                        mm(ps[:, :], WSM[:, a:a + 128], pts[ic][0][:, :], ic == 0, ic == 1,
                           ["wsm", pts[ic][1]], [pk])
                    act(UY[:, c, :], ps[:, :], AF.Copy, [pk] + PVK(l), [("UY", c)],
                        scale=PV[:, l, C_PS + c:C_PS + c + 1])

        def lru_branch(l, j):
            for g in range(4):
                lru_unit(l, j, g)

        def lru_unit(l, j, g):
            if True:
                slot, skey = ws.get(WIN[l, 4 + g], ("scr", "win", l, 4 + g), 2048)
                load_wsm(WBD[l, g], ("scr", "wbd", l, g))
                for cc in range(2):
                    c = 2 * g + cc
                    ps, pk = proj(slot, skey, cc * 128, 256, HTc, HTk)
                    ub, uk = SCr.next()
                    cp("act", ub[:, 4:516], ps[:, :], [pk], [uk])
                    if j == 0:
                        memset("dve", ub[:, 0:4], 0.0, [uk])
                    else:
                        cp("dve", ub[:, 0:4], ULH[:, c, :], [("ulh", c)], [uk])
                    cp("dve", ULH[:, c, :], ub[:, 512:516], [uk], [("ulh", c)])
                    xc, xk = SCr.next()
                    cw = lambda k, c=c: PV[:, l, C_CW + 8 * k + c:C_CW + 8 * k + c + 1]
                    ts("dve", xc[:, 0:T], ub[:, 4:516], cw(3), PV[:, l, C_CB + c:C_CB + c + 1], ALU.mult, ALU.add,
                       [uk] + PVK(l), [xk])
                    for k in range(3):
                        stt(xc[:, 0:T], ub[:, 1 + k:1 + k + T], cw(k), xc[:, 0:T], ALU.mult, ALU.add,
                            [uk, xk] + PVK(l), [xk])
                    xb, xbk = SBr.next()
                    cp("dve", xb[:, :], xc[:, 0:T], [xk], [xbk])
                    pr, prk = PG.next()
                    a = 0 * 256 + cc * 128
                    mm(pr[:, :], WSM[:, a:a + 128], xb[:, :], True, True, ["wsm", xbk], [prk])
                    pi, pik = PG.next()
                    a = 1 * 256 + cc * 128
                    mm(pi[:, :], WSM[:, a:a + 128], xb[:, :], True, True, ["wsm", xbk], [pik])
                    r_, rk = SCr.next()
                    i_, ik = SCr.next()
                    a_, ak = SCr.next()
                    m_, mk = SCr.next()
                    sigmoid3(r_[:, 0:T], pr[:, :], PV[:, l, C_BRG + c:C_BRG + c + 1], [prk], rk, PVK(l))
                    sigmoid3(i_[:, 0:T], pi[:, :], PV[:, l, C_BIG + c:C_BIG + c + 1], [pik], ik, PVK(l))
                    act(a_[:, 0:T], r_[:, 0:T], AF.Exp, [rk] + PVK(l), [ak], scale=PV[:, l, C_NSP + c:C_NSP + c + 1])
                    act(m_[:, 0:T], r_[:, 0:T], AF.Exp, [rk] + PVK(l), [mk], scale=PV[:, l, C_NSP2 + c:C_NSP2 + c + 1])
                    act(m_[:, 0:T], m_[:, 0:T], AF.Ln, [mk], [mk], bias=1.0, scale=-1.0)
                    act(m_[:, 0:T], m_[:, 0:T], AF.Exp, [mk], [mk], scale=0.5)
                    tt("pool", i_[:, 0:T], i_[:, 0:T], xc[:, 0:T], ALU.mult, [ik, xk], [ik])
                    tt("pool", i_[:, 0:T], i_[:, 0:T], m_[:, 0:T], ALU.mult, [ik, mk], [ik])
                    init = 0.0 if j == 0 else HST[:, c:c + 1]
                    P.add("dve", lambda e, r_=r_, a_=a_, i_=i_, init=init: e.tensor_tensor_scan(
                        r_[:, 0:T], a_[:, 0:T], i_[:, 0:T], init, ALU.mult, ALU.add),
                        [ak, ik, ("hst", c)], [rk], occ=160.0 + T * 2.1)
                    cp("dve", HST[:, c:c + 1], r_[:, T - 1:T], [rk], [("hst", c)])
                    cp("dve", UY[:, 8 + c, :], r_[:, 0:T], [rk], [("UY", 8 + c)])

        def attn_branch(l, j):
            tok0 = j * T
            for hp in range(4):
                for which in range(2):
                    g = 8 + 4 * which + hp
                    slot, skey = ws.get(WIN[l, g], ("scr", "win", l, g), 2048)
                    for hh in range(2):
                        hd = 2 * hp + hh
                        ps, pk = proj(slot, skey, hh * 128, 256, HTc, HTk)
                        sq, sqk = SBr.next()
                        act(sq[:, :], ps[:, :], AF.Square, [pk], [sqk])
                        p2, p2k = PG.next()
                        mm(p2[:, :], ONESB, sq[:, :], True, True, [sqk, "c_onesb"], [p2k])
                        rsb, rsk_ = SCr.next()
                        rs = rsb[:, 0:T]
                        act(rs, p2[:, :], AF.Ln, [p2k], [rsk_], bias=EPS, scale=1.0 / 128)
                        act(rs, rs, AF.Exp, [rsk_], [rsk_], scale=-0.5)
                        if which == 0:
                            stt(PQ[:, hd % 4, :], ps[:, :], PV[:, l, C_QG:C_QG + 1], rs, ALU.mult, ALU.mult,
                                [pk, rsk_] + PVK(l), [("PQ", hd % 4)])
                        else:
                            stt(KT[:, hd, tok0:tok0 + T], ps[:, :], PV[:, l, C_KN:C_KN + 1], rs, ALU.mult, ALU.mult,
                                [pk, rsk_] + PVK(l), [("KT", hd, j)])
                vg = hp
                g = 16 + vg
                slot, skey = ws.get(WIN[l, g], ("scr", "win", l, g), 2048)
                for tb in range(4):
                    ps, pk = PG.next()
                    for kc in range(8):
                        mm(ps[:, 0:256], HT[:, kc, tb * 128:(tb + 1) * 128], wsl(slot, kc * 256, kc * 256 + 256),
                           kc == 0, kc == 7, [skey, ("HT", kc)], [pk])
                    eng = "act" if tb % 2 == 0 else "dve"
                    cp(eng, VC[:, 4 * j + tb, vg * 256:(vg + 1) * 256], ps[:, 0:256], [pk], [("VC", 4 * j + tb, vg)])
                for hh in range(2):
                    scores(l, j, 2 * hp + hh)

        def scores(l, j, hd):
            if True:
                po, pok = PO.next()
                rsum, rsk = RSUM[:, 0, :], ("rsum", 0)
                memset("pool", rsum[:, :], 0.0, [rsk])
                first = True
                for kb in range(4 * j + 3, -1, -1):
                    diag = kb >= 4 * j
                    off = (kb - 4 * j) * 128 if diag else 0
                    N = T - off
                    pz, pzk = PZ.next()
                    kblk = KT[:, hd, kb * 128:(kb + 1) * 128]
                    zr = [("KT", hd, kb // 4), ("PQ", hd % 4)]
                    if not diag:
                        mm(pz[:, 0:N], kblk, PQ[:, hd % 4, off:T], True, True, zr, [pzk])
                    else:
                        mm(pz[:, 0:128], kblk, PQ[:, hd % 4, off:off + 128], True, False, zr, [pzk])
                        mm(pz[:, 0:128], IDENTB, NEGB, False, True, ["c_idb", "c_negb"], [pzk])
                        if N > 128:
                            mm(pz[:, 128:N], kblk, PQ[:, hd % 4, off + 128:T], True, True, zr, [pzk])
                    s_, sk = SCr.next()
                    act(s_[:, 0:N], pz[:, 0:N], AF.Exp, [pzk], [sk], scale=-1.0)
                    act(s_[:, 0:N], s_[:, 0:N], AF.Ln, [sk], [sk], bias=1.0)
                    nl, nlk = SBr.next()
                    tt("dve", nl[:, 0:N], s_[:, 0:N], pz[:, 0:N], ALU.add, [sk, pzk], [nlk])
                    pa, pak = PA.next()
                    mm(pa[:, 0:N], UMAT, nl[:, 0:N], True, first, [nlk, "c_umat"], [pak])
                    if not first:
                        mm(pa[:, 0:N], ONESB, rsum[:, off:T], False, True, [rsk, "c_onesb"], [pak])
                    tt("dve", s_[:, 0:N], s_[:, 0:N], pa[:, 0:N], ALU.add, [sk, pak], [sk])
                    w_, wk = SBr.next()
                    act(w_[:, 0:N], s_[:, 0:N], AF.Exp, [sk], [wk], scale=-1.0)
                    mm(po[:, off:T], VC[:, kb, hd * 128:(hd + 1) * 128], w_[:, 0:N], first, kb == 0,
                       [("VC", kb, hd // 2), wk], [pok], skip=True)
                    if kb > 0:
                        tt("pool", rsum[:, off:T], rsum[:, off:T], nl[:, 0:N], ALU.add, [rsk, nlk], [rsk])
                    first = False
                cp("act", UY[:, 16 + hd, :], po[:, :], [pok], [("UY", 16 + hd)])
                merge01(l, j, hd)

        def gate_branch(l, oc, b, PR):
            slot, skey = ws.get(WMG[l, oc, b], ("scr", "wmg", l, oc, b), 2048)
            pg, pgk = PR.next()
            pb, pbk = PR.next()
            for kc in range(8):
                a = kc * 256
                mm(pg[:, :], wsl(slot, a, a + 128), HT[:, kc, :], kc == 0, kc == 7, [skey, ("HT", kc)], [pgk])
            for kc in range(8):
                a = kc * 256 + 128
                mm(pb[:, :], wsl(slot, a, a + 128), UY[:, 8 * b + kc, :], kc == 0, kc == 7,
                   [skey, ("UY", 8 * b + kc)], [pbk])
            gt, gk = SCr.next()
            sigmoid3(gt[:, 0:T], pg[:, :], PV[:, l, C_BG + 8 * b + oc:C_BG + 8 * b + oc + 1], [pgk], gk, PVK(l))
            tt("dve", gt[:, 0:T], gt[:, 0:T], pb[:, :], ALU.mult, [gk, pbk], [gk])
            return gt, gk

        def merge01(l, j, oc):
            g0, g0k = gate_branch(l, oc, 0, PX)
            g1, g1k = gate_branch(l, oc, 1, PX)
            tt("pool", MB[:, oc, :], g0[:, 0:T], g1[:, 0:T], ALU.add, [g0k, g1k], [("MB", oc)])

        def merge_and_out(l, j):
            tok = slice(j * T, (j + 1) * T)
            for oc in range(8):
                g2, g2k = gate_branch(l, oc, 2, PG)
                tt("pool", MB[:, oc, :], MB[:, oc, :], g2[:, 0:T], ALU.add, [("MB", oc), g2k], [("MB", oc)])
            for g in range(4):
                slot, skey = ws.get(WOU[l, g], ("scr", "wou", l, g), 2048)
                for cc in range(2):
                    oc = 2 * g + cc
                    ps, pk = proj(slot, skey, cc * 128, 256, lambda kc: MB[:, kc, :], lambda kc: ("MB", kc))
                    tt("dve", XT[:, oc, tok], XT[:, oc, tok], ps[:, :], ALU.add, [("XT", oc, j), pk], [("XT", oc, j)])

        def ffn_steps(l, j):
            tok = slice(j * T, (j + 1) * T)
            steps = []

            def gu(fc, fi):
                slot, skey = ws.get(WGU[l, fc], ("scr", "wgu", l, fc), 2048)
                pg, pgk = proj(slot, skey, 0, 256, H2c, H2k)
                pu, puk = proj(slot, skey, 128, 256, H2c, H2k)
                sg, sgk = SCr.next()
                act(sg[:, 0:T], pg[:, :], AF.Exp, [pgk], [sgk], scale=-1.0)
                act(sg[:, 0:T], sg[:, 0:T], AF.Ln, [sgk], [sgk], bias=1.0)
                act(sg[:, 0:T], sg[:, 0:T], AF.Exp, [sgk], [sgk], scale=-1.0)
                tt("dve", sg[:, 0:T], sg[:, 0:T], pg[:, :], ALU.mult, [sgk, pgk], [sgk])
                tt("dve", ATc(fi), sg[:, 0:T], pu[:, :], ALU.mult, [sgk, puk], [ATk(fi)])

            def dn(oc, p_):
                ps, pk = PG.next()
                slot, skey = ws.get(WDN[l, oc, p_], ("scr", "wdn", l, oc, p_), 1408)
                for kk in range(11):
                    mm(ps[:, :], wsl(slot, kk * 128, kk * 128 + 128), ATc(kk),
                       kk == 0, kk == 10, [skey, ATk(kk)], [pk])
                tt("dve", XT[:, oc, tok], XT[:, oc, tok], ps[:, :], ALU.add, [("XT", oc, j), pk], [("XT", oc, j)])

            for p_ in range(2):
                for fi in range(11):
                    steps.append(lambda fc=11 * p_ + fi, fi=fi: gu(fc, fi))
                for oc in range(8):
                    steps.append(lambda oc=oc, p_=p_: dn(oc, p_))
            return steps

        def ffn(l, j):
            for st_ in ffn_steps(l, j):
                st_()

        def early(l, j):
            P.phase = 'rmsnorm'
            rmsnorm(l, j, C_NM)
            P.phase = 'pool_branch'
            pool_branch(l, j)
            P.phase = 'lru_branch'
            lru_branch(l, j)

        def late1(l, j):
            P.phase = 'attn_branch'
            attn_branch(l, j)
            P.phase = 'merge_and_out'
            merge_and_out(l, j)
            P.phase = 'rmsnorm2'
            rmsnorm(l, j, C_NF, H2c, H2k)

        def late2(s, l, j):
            P.phase = 'ffn'
            ffn(l, j)
            if l == NL - 1:
                P.phase = 'store_y'
                store_y(s, j)

        for s in range(NSEQ):
            P.phase = 'load'
            load_x(s)
            stages = [(l, j) for l in range(NL) for j in range(NT)]
            if PIPE:
                early(*stages[0])
                for i, (l, j) in enumerate(stages):
                    late1(l, j)
                    fsteps = ffn_steps(l, j)
                    esteps = []
                    if i + 1 < len(stages):
                        l2, j2 = stages[i + 1]
                        P.phase = 'rmsnorm'
                        rmsnorm(l2, j2, C_NM)
                        pus = [('pool_branch', lambda g=g: pool_unit(l2, j2, g)) for g in range(4)]
                        lus = [('lru_branch', lambda g=g: lru_unit(l2, j2, g)) for g in range(4)]
                        if EORDER == 0:
                            esteps = pus + lus
                        elif EORDER == 1:
                            esteps = lus + pus
                        else:
                            esteps = [u_ for pr_ in zip(lus, pus) for u_ in pr_]
                    for k, fs in enumerate(fsteps):
                        if k % ESTRIDE == 0 and esteps:
                            ph, es_ = esteps.pop(0)
                            P.phase = ph
                            es_()
                        P.phase = 'ffn'
                        fs()
                    for ph, es_ in esteps:
                        P.phase = ph
                        es_()
                    if l == NL - 1:
                        P.phase = 'store_y'
                        store_y(s, j)
            else:
                for (l, j) in stages:
                    early(l, j)
                    late1(l, j)
                    late2(s, l, j)
        P.add("sp", None, [("yout", s, j, tb) for s in range(NSEQ) for j in range(NT) for tb in range(4)], ())

    saved = (P.ops, P.pages, list(wcount), list(stcnt), stg_i[0], wsmcnt[0])
    P.ops, P.pages = [], {}
    rots = [PG, PZ, PA, PO, PX]
    rsave = [r.i for r in rots]
    ws1 = WStream(None)
    record_all(ws1)
    log = ws1.req
    P.ops, P.pages = saved[0], saved[1]
    wcount[:] = saved[2]
    stcnt[:] = saved[3]
    stg_i[0] = saved[4]
    wsmcnt[0] = saved[5]
    for r, i in zip(rots, rsave):
        r.i = i
    record_all(WStream(log))

    if speed:
        for o_ in P.ops:
            if o_.eng in speed and not o_.is_dma:
                o_.occ *= speed[o_.eng]
                o_.lat *= speed[o_.eng]
    P.sync_lat = sync_lat
    if FILL:
        fill_rhs = CONB[:, :, :].rearrange("p a b -> p (a b)")
        fill_ps = psum[FILL_BANK]
        P.filler = (FILL[0], FILL[1], FILL[2],
                    lambda e: e.matmul(fill_ps[:, :], ONESB, fill_rhs, start=True, stop=True))
    P.use_blevel = blevel
    P.finalize(new_sem)
    if not emit:
        es.close()
        return nc, P
    with nc.Block() as block:
        @block.tensor
        def _(e):
            P.emit("pe", e)

        @block.scalar
        def _(e):
            P.emit("act", e)

        @block.vector
        def _(e):
            P.emit("dve", e)

        @block.gpsimd
        def _(e):
            P.emit("pool", e)

        @block.sync
        def _(e):
            P.emit("sp", e)
    es.close()
    return nc, P


_CACHE = {}


def kernel(**inputs):
    ncores = 8
    if "nc" not in _CACHE:
        _CACHE["nc"] = build()[0]
    nc = _CACHE["nc"]
    x = np.ascontiguousarray(np.asarray(inputs["x"], dtype=np.float32))
    B = x.shape[0]
    per = B // ncores
    in_maps = []
    for c in range(ncores):
        m = {}
        for k, v in inputs.items():
            if k == "x":
                m["x"] = np.ascontiguousarray(x[c * per:(c + 1) * per])
            else:
                m[k] = np.ascontiguousarray(np.asarray(v, dtype=np.float32))
        in_maps.append(m)
    res = run_bass_kernel_spmd(nc, in_maps, core_ids=list(range(ncores)))
    out = np.concatenate([np.asarray(r["y"]) for r in res.results], axis=0)
    return out.astype(np.float32)
```
